# Optimizing a Trainium2 kernel written in Bass

```python
import math
import jax, jax.numpy as jnp
from jax import lax
import numpy as np

D_MODEL = 1024
BATCH = 16
SEQ = 4096
DEPTH = 4

PLE_DIM = 256
ROPE_THETA = 500000.0
NORM_EPS = 1e-6
Q_BLOCK = 128

DA_HEADS = 4
DA_DQK = 64
DA_DV = 2 * DA_DQK
DA_ROT = DA_DQK // 4

SSD_HEADS = 8
SSD_HEADDIM = 64
SSD_DINNER = SSD_HEADS * SSD_HEADDIM
SSD_GROUPS = 2
SSD_DSTATE = 128
SSD_CONV = 4
SSD_CHUNK = 128
SSD_CONV_DIM = SSD_DINNER + 2 * SSD_GROUPS * SSD_DSTATE

GLA_HEADS = 4
GLA_DK = 64
GLA_DV = 128
GLA_GATE_RANK = 16
GLA_GATE_NORMALIZER = 16.0
GLA_CHUNK = 64

DA_WIDTH = DA_HEADS * DA_DV
GLA_WIDTH = GLA_HEADS * GLA_DV
MIX_WIDTH = DA_WIDTH + SSD_DINNER + GLA_WIDTH
D_FF = -(-8 * D_MODEL // (3 * 256)) * 256

IN_SIZES = (
    DA_HEADS * 2 * DA_DQK,
    DA_HEADS * 2 * DA_DQK,
    DA_WIDTH,
    SSD_DINNER,
    SSD_CONV_DIM,
    SSD_HEADS,
    GLA_HEADS * GLA_DK,
    GLA_HEADS * GLA_DK,
    GLA_WIDTH,
    GLA_WIDTH,
    GLA_GATE_RANK,
)
IN_COLS = sum(IN_SIZES)
IN_SPLITS = tuple(int(v) for v in np.cumsum(IN_SIZES)[:-1])

kernel_name = 'hymba_style_diffattn_ssd_gla_trunk'


def rms_norm(x, gain):
    xf = x.astype(jnp.float32)
    y = xf * lax.rsqrt(jnp.mean(xf * xf, axis=-1, keepdims=True) + NORM_EPS)
    return (y * gain.astype(jnp.float32)).astype(x.dtype)


def rope_tables(positions):
    inv_freq = ROPE_THETA ** (-jnp.arange(0, DA_ROT, 2, dtype=jnp.float32) / DA_ROT)
    ang = positions.astype(jnp.float32)[..., None] * inv_freq
    return jnp.cos(ang), jnp.sin(ang)


def partial_rope(x, cos, sin):
    half = DA_ROT // 2
    x1 = x[..., :half].astype(jnp.float32)
    x2 = x[..., half:DA_ROT].astype(jnp.float32)
    c = cos[:, :, None, :]
    s = sin[:, :, None, :]
    r1 = (x1 * c - x2 * s).astype(x.dtype)
    r2 = (x2 * c + x1 * s).astype(x.dtype)
    return jnp.concatenate([r1, r2, x[..., DA_ROT:]], axis=-1)


def diff_attention(q, k, v, positions, g_q, g_k, lq1, lk1, lq2, lk2, g_sub, lambda_init):
    B, S = q.shape[:2]
    q = rms_norm(q.reshape(B, S, DA_HEADS * 2, DA_DQK), g_q)
    k = rms_norm(k.reshape(B, S, DA_HEADS * 2, DA_DQK), g_k)
    v = v.reshape(B, S, DA_HEADS, DA_DV)
    cos, sin = rope_tables(positions)
    q = partial_rope(q, cos, sin).reshape(B, S, DA_HEADS, 2, DA_DQK)
    k = partial_rope(k, cos, sin).reshape(B, S, DA_HEADS, 2, DA_DQK)
    lam = (jnp.exp(jnp.sum(lq1.astype(jnp.float32) * lk1.astype(jnp.float32)))
           - jnp.exp(jnp.sum(lq2.astype(jnp.float32) * lk2.astype(jnp.float32)))
           + lambda_init)
    scale = DA_DQK ** -0.5
    n_blk = S // Q_BLOCK
    qb = q.reshape(B, n_blk, Q_BLOCK, DA_HEADS, 2, DA_DQK).transpose(1, 0, 2, 3, 4, 5)
    key_idx = jnp.arange(S)

    def one_block(args):
        qi, bi = args
        s = jnp.einsum('bqhcd,bkhcd->bhcqk', qi, k,
                       preferred_element_type=jnp.float32) * scale
        q_idx = bi * Q_BLOCK + jnp.arange(Q_BLOCK)
        causal = key_idx[None, :] <= q_idx[:, None]
        s = jnp.where(causal, s, -jnp.inf)
        pr = jax.nn.softmax(s, axis=-1)
        attn = pr[:, :, 0] - lam * pr[:, :, 1]
        return jnp.einsum('bhqk,bkhd->bqhd', attn.astype(v.dtype), v)

    out = lax.map(one_block, (qb, jnp.arange(n_blk)))
    out = out.transpose(1, 0, 2, 3, 4).reshape(B, S, DA_HEADS, DA_DV)
    out = rms_norm(out, g_sub) * (1.0 - lambda_init)
    return out.reshape(B, S, DA_WIDTH)


def ssd_mixer(z, xbc, dt_raw, conv_w, conv_b, dt_bias, a_log, d_skip, g_norm):
    B, S = z.shape[:2]
    G, R, P, N, L = SSD_GROUPS, SSD_HEADS // SSD_GROUPS, SSD_HEADDIM, SSD_DSTATE, SSD_CHUNK
    nc = S // L
    xbc = lax.conv_general_dilated(
        xbc, conv_w[:, None, :], window_strides=(1,), padding=[(SSD_CONV - 1, 0)],
        dimension_numbers=('NWC', 'WIO', 'NWC'), feature_group_count=SSD_CONV_DIM) + conv_b
    xbc = jax.nn.silu(xbc)
    xs, bs, cs = jnp.split(xbc, [SSD_DINNER, SSD_DINNER + G * N], axis=-1)
    dt = jax.nn.softplus(dt_raw.astype(jnp.float32) + dt_bias.astype(jnp.float32))
    a = -jnp.exp(a_log.astype(jnp.float32))
    xh = xs.astype(jnp.float32).reshape(B, nc, L, G, R, P)
    xdt = xh * dt.reshape(B, nc, L, G, R)[..., None]
    bm = bs.astype(jnp.float32).reshape(B, nc, L, G, N)
    cm = cs.astype(jnp.float32).reshape(B, nc, L, G, N)
    cum = jnp.cumsum((dt * a).reshape(B, nc, L, G, R), axis=2)
    seg = cum[:, :, :, None] - cum[:, :, None, :]
    causal = jnp.tril(jnp.ones((L, L), dtype=bool))[:, :, None, None]
    decay = jnp.exp(jnp.where(causal, seg, -jnp.inf))
    cb = jnp.einsum('bclgn,bcsgn->bclsg', cm, bm)
    y_diag = jnp.einsum('bclsgr,bcsgrp->bclgrp', cb[..., None] * decay, xdt)
    decay_to_end = jnp.exp(cum[:, :, -1:] - cum)
    states = jnp.einsum('bclgn,bclgrp->bcgrpn', bm, xdt * decay_to_end[..., None])
    chunk_decay = jnp.exp(cum[:, :, -1])

    def step(h, inp):
        st, dec = inp
        return dec[..., None, None] * h + st, h

    h0 = jnp.zeros((B, G, R, P, N), jnp.float32)
    _, prev = lax.scan(step, h0, (states.transpose(1, 0, 2, 3, 4, 5),
                                  chunk_decay.transpose(1, 0, 2, 3)))
    prev = prev.transpose(1, 0, 2, 3, 4, 5)
    y_off = jnp.einsum('bclgn,bcgrpn->bclgrp', cm, prev) * jnp.exp(cum)[..., None]
    y = y_diag + y_off + d_skip.astype(jnp.float32).reshape(G, R)[:, :, None] * xh
    y = y.reshape(B, S, SSD_DINNER) * jax.nn.silu(z.astype(jnp.float32))
    y = rms_norm(y.reshape(B, S, G, SSD_DINNER // G), g_norm.reshape(G, SSD_DINNER // G))
    return y.reshape(B, S, SSD_DINNER).astype(z.dtype)


def gla_mixer(q, k, v, g_out, gate_lr, w_gate2, b_gate, g_norm):
    B, S = q.shape[:2]
    H, DK, DV, L = GLA_HEADS, GLA_DK, GLA_DV, GLA_CHUNK
    nc = S // L
    gk = jax.nn.log_sigmoid((gate_lr @ w_gate2 + b_gate).astype(jnp.float32)) / GLA_GATE_NORMALIZER
    qf = q.astype(jnp.float32).reshape(B, nc, L, H, DK) * (DK ** -0.5)
    kf = k.astype(jnp.float32).reshape(B, nc, L, H, DK)
    vf = v.astype(jnp.float32).reshape(B, nc, L, H, DV)
    bcum = jnp.cumsum(gk.reshape(B, nc, L, H, DK), axis=2)
    q_dec = qf * jnp.exp(bcum)
    k_inv = kf * jnp.exp(-bcum)
    att = jnp.einsum('bclhd,bcshd->bchls', q_dec, k_inv)
    att = jnp.where(jnp.tril(jnp.ones((L, L), dtype=bool)), att, 0.0)
    o_intra = jnp.einsum('bchls,bcshv->bclhv', att, vf)
    k_end = kf * jnp.exp(bcum[:, :, -1:] - bcum)
    states = jnp.einsum('bclhd,bclhv->bchdv', k_end, vf)
    chunk_decay = jnp.exp(bcum[:, :, -1])

    def step(st_prev, inp):
        st, dec = inp
        return dec[..., None] * st_prev + st, st_prev

    s0 = jnp.zeros((B, H, DK, DV), jnp.float32)
    _, prev = lax.scan(step, s0, (states.transpose(1, 0, 2, 3, 4),
                                  chunk_decay.transpose(1, 0, 2, 3)))
    prev = prev.transpose(1, 0, 2, 3, 4)
    o_inter = jnp.einsum('bclhd,bchdv->bclhv', q_dec, prev)
    o = (o_intra + o_inter).reshape(B, S, H, DV)
    o = rms_norm(o, g_norm) * jax.nn.silu(g_out.astype(jnp.float32).reshape(B, S, H, DV))
    return o.reshape(B, S, GLA_WIDTH).astype(q.dtype)


def setup_inputs(seed: int = 0) -> dict:
    key = jax.random.key(seed)
    ks = iter(jax.random.split(key, 48))

    def nrm(shape, scale):
        return jax.random.normal(next(ks), shape, jnp.float32) * scale

    def gain(shape):
        return 1.0 + nrm(shape, 0.02)

    x = nrm((BATCH, SEQ, D_MODEL), 1.0)
    p = nrm((DEPTH, BATCH, SEQ, PLE_DIM), 1.0)
    positions = jnp.broadcast_to(jnp.arange(SEQ, dtype=jnp.int32)[None, :], (BATCH, SEQ))
    dt0 = jnp.exp(jax.random.uniform(next(ks), (DEPTH, SSD_HEADS), jnp.float32)
                  * (math.log(0.1) - math.log(0.001)) + math.log(0.001))
    ssd_dt_bias = dt0 + jnp.log(-jnp.expm1(-dt0))
    ssd_a_log = jnp.log(jax.random.uniform(next(ks), (DEPTH, SSD_HEADS), jnp.float32,
                                           minval=1.0, maxval=16.0))
    return {
        'x': x,
        'p': p,
        'positions': positions,
        'attn_norm': gain((DEPTH, D_MODEL)),
        'w_in': nrm((DEPTH, D_MODEL, IN_COLS), D_MODEL ** -0.5),
        'da_q_norm': gain((DEPTH, DA_DQK)),
        'da_k_norm': gain((DEPTH, DA_DQK)),
        'da_lambda_q1': nrm((DEPTH, DA_DQK), 0.1),
        'da_lambda_k1': nrm((DEPTH, DA_DQK), 0.1),
        'da_lambda_q2': nrm((DEPTH, DA_DQK), 0.1),
        'da_lambda_k2': nrm((DEPTH, DA_DQK), 0.1),
        'da_sub_norm': gain((DEPTH, DA_DV)),
        'ssd_conv_w': nrm((DEPTH, SSD_CONV, SSD_CONV_DIM), SSD_CONV ** -0.5),
        'ssd_conv_b': nrm((DEPTH, SSD_CONV_DIM), 0.02),
        'ssd_dt_bias': ssd_dt_bias,
        'ssd_a_log': ssd_a_log,
        'ssd_d': 1.0 + nrm((DEPTH, SSD_HEADS), 0.1),
        'ssd_norm': gain((DEPTH, SSD_DINNER)),
        'gla_w_gate2': nrm((DEPTH, GLA_GATE_RANK, GLA_HEADS * GLA_DK), GLA_GATE_RANK ** -0.5),
        'gla_b_gate': nrm((DEPTH, GLA_HEADS * GLA_DK), 0.02),
        'gla_norm': gain((DEPTH, GLA_DV)),
        'w_out': nrm((DEPTH, MIX_WIDTH, D_MODEL), MIX_WIDTH ** -0.5),
        'ffn_norm': gain((DEPTH, D_MODEL)),
        'w_ffn_gate': nrm((DEPTH, D_MODEL, D_FF), D_MODEL ** -0.5),
        'w_ffn_up': nrm((DEPTH, D_MODEL, D_FF), D_MODEL ** -0.5),
        'w_ffn_down': nrm((DEPTH, D_FF, D_MODEL), D_FF ** -0.5),
        'ple_w_proj': nrm((DEPTH, PLE_DIM, D_MODEL), PLE_DIM ** -0.5),
        'ple_w_gate': nrm((DEPTH, D_MODEL, D_MODEL), D_MODEL ** -0.5),
    }


def reference(x, p, positions, attn_norm, w_in, da_q_norm, da_k_norm, da_lambda_q1,
              da_lambda_k1, da_lambda_q2, da_lambda_k2, da_sub_norm, ssd_conv_w, ssd_conv_b,
              ssd_dt_bias, ssd_a_log, ssd_d, ssd_norm, gla_w_gate2, gla_b_gate, gla_norm,
              w_out, ffn_norm, w_ffn_gate, w_ffn_up, w_ffn_down, ple_w_proj, ple_w_gate):
    h = x
    for i in range(DEPTH):
        lambda_init = 0.8 - 0.6 * math.exp(-0.3 * i)
        a = rms_norm(h, attn_norm[i])
        proj = a @ w_in[i]
        (da_q, da_k, da_v, ssd_z, ssd_xbc, ssd_dt,
         gla_q, gla_k, gla_v, gla_g, gla_lr) = jnp.split(proj, IN_SPLITS, axis=-1)
        y_da = diff_attention(da_q, da_k, da_v, positions, da_q_norm[i], da_k_norm[i],
                              da_lambda_q1[i], da_lambda_k1[i], da_lambda_q2[i],
                              da_lambda_k2[i], da_sub_norm[i], lambda_init)
        y_ssd = ssd_mixer(ssd_z, ssd_xbc, ssd_dt, ssd_conv_w[i], ssd_conv_b[i],
                          ssd_dt_bias[i], ssd_a_log[i], ssd_d[i], ssd_norm[i])
        y_gla = gla_mixer(gla_q, gla_k, gla_v, gla_g, gla_lr, gla_w_gate2[i],
                          gla_b_gate[i], gla_norm[i])
        mix = jnp.concatenate([y_da, y_ssd.astype(y_da.dtype), y_gla.astype(y_da.dtype)], axis=-1)
        h = h + mix @ w_out[i]
        f = rms_norm(h, ffn_norm[i])
        h = h + (jax.nn.silu(f @ w_ffn_gate[i]) * (f @ w_ffn_up[i])) @ w_ffn_down[i]
        h = h + jax.nn.sigmoid(h @ ple_w_gate[i]) * (p[i] @ ple_w_proj[i])
    return h
```

```python
import math
import numpy as np
import ml_dtypes
import concourse.bass as bass
import concourse.mybir as mybir
from concourse.bass_utils import run_bass_kernel_spmd

F32, BF16, I32 = mybir.dt.float32, mybir.dt.bfloat16, mybir.dt.int32
AF = mybir.ActivationFunctionType
ALU = mybir.AluOpType
AX = mybir.AxisListType

D = 1024
IN_COLS = 4632
DFF = 2816
EPS = 1e-6
NDS = 56
import os
DA_STOP = int(os.environ.get('DA_STOP', '0'))
DA_SKIP = os.environ.get('DA_SKIP', '')
DA_CUT = int(os.environ.get('DA_CUT', '9'))
GLA_CUT = int(os.environ.get('GLA_CUT', '9'))
POOL_TT = os.environ.get('POOL_TT', '1') == '1'
SEQ_SKEW = int(os.environ.get('SEQ_SKEW', '2'))
FUSE_GLA = os.environ.get('FUSE_GLA', '1') == '1'
GLA_SUB = int(os.environ.get('GLA_SUB', '9'))


class Res:
    __slots__ = ("w", "r")

    def __init__(self):
        self.w = None
        self.r = {}


class Tile:
    def __init__(self, ap):
        self.ap = ap
        self.res = Res()
        self.dsem = None

    def __getitem__(self, k):
        return self.ap[k]


class KB:
    COMP = ("pe", "act", "dve", "pool")
    ENGS = ("pe", "act", "dve", "pool", "sp")

    def __init__(self, nc):
        self.nc = nc
        self.ops = {e: [] for e in self.ENGS}
        self.sem = {e: nc.alloc_semaphore("s_" + e) for e in self.COMP}
        self.cnt = {e: 0 for e in self.COMP}
        self.seen = {e: {} for e in self.ENGS}
        self.dsems = [[nc.alloc_semaphore("d%d" % i), 0] for i in range(NDS)]
        self.dnext = 0
        self.tiles = []
        self.nops = 0

    def track(self, t):
        self.tiles.append(t)
        return t

    def _filter(self, eng, deps):
        best = {}
        for (s, v) in deps:
            if eng == "pe" and s is self.sem["pe"]:
                continue
            if self.seen[eng].get(s.num, 0) >= v:
                continue
            if best.get(s.num, (None, 0))[1] < v:
                best[s.num] = (s, v)
        for k, (s, v) in best.items():
            self.seen[eng][k] = v
        return list(best.values())

    def _deps(self, rd, wr):
        deps = []
        for t in rd:
            if t.res.w:
                deps.append(t.res.w)
        for t in wr:
            if t.res.w:
                deps.append(t.res.w)
            deps.extend(t.res.r.values())
        return deps

    def _commit(self, ev, rd, wr):
        wrs = set(id(t) for t in wr)
        for t in rd:
            if id(t) in wrs:
                continue
            cur = t.res.r.get(ev[0].num)
            if cur is None or cur[1] < ev[1]:
                t.res.r[ev[0].num] = ev
        for t in wr:
            t.res.w = ev
            t.res.r = {}

    def op(self, eng, fn, rd=(), wr=(), sig=True):
        waits = self._filter(eng, self._deps(rd, wr))
        if sig:
            self.cnt[eng] += 1
            ev = (self.sem[eng], self.cnt[eng])
            inc = (self.sem[eng], 1)
        else:
            ev = (self.sem[eng], self.cnt[eng] + 1)
            inc = None
        self.ops[eng].append((waits, fn, inc))
        self._commit(ev, rd, wr)
        self.nops += 1

    def dma(self, pairs, slot, rd=(), wr=(), q="sp", slow=False):
        if not isinstance(pairs, list):
            pairs = [pairs]
        if slot.dsem is None:
            slot.dsem = self.dsems[self.dnext % NDS]
            self.dnext += 1
        d = slot.dsem
        deps = self._deps(rd, wr)
        if d[1] > 0:
            deps.append((d[0], d[1]))
        waits = self._filter(q, deps)
        for (o, i) in pairs:
            if slow:
                fn = (lambda e, o=o, i=i: e.dma_start(out=o, in_=i, allow_slow_non_contiguous=True))
            else:
                fn = (lambda e, o=o, i=i: e.dma_start(out=o, in_=i))
            self.ops[q].append((waits, fn, (d[0], 16)))
            waits = []
            d[1] += 16
        self._commit((d[0], d[1]), rd, wr)
        self.nops += len(pairs)

    def barrier(self):
        evs = [(self.sem[e], self.cnt[e]) for e in self.COMP if self.cnt[e] > 0]
        evs += [(d[0], d[1]) for d in self.dsems if d[1] > 0]
        for e in self.ENGS:
            waits = self._filter(e, evs)
            if waits:
                self.ops[e].append((waits, None, None))
        for t in self.tiles:
            t.res.w = None
            t.res.r = {}
            t.dsem = None
        self.tiles = []

    def emit(self):
        def mk(name):
            def body(e):
                for waits, fn, inc in self.ops[name]:
                    for (s, v) in waits:
                        e.wait_ge(s, v)
                    if fn is not None:
                        ins = fn(e)
                        if inc is not None:
                            ins.then_inc(inc[0], inc[1])
            return body

        with self.nc.Block() as block:
            block.tensor(mk("pe"))
            block.scalar(mk("act"))
            block.vector(mk("dve"))
            block.gpsimd(mk("pool"))
            block.sync(mk("sp"))

    def mm(self, out, lhsT, rhs, start, stop, rd, wr, sig=None):
        if sig is None:
            sig = stop
        self.op("pe", lambda e: e.matmul(out, lhsT, rhs, start=start, stop=stop), rd, wr, sig)

    def tr(self, out, in_, ident, rd, wr):
        self.op("pe", lambda e: e.transpose(out, in_, ident), rd, wr)

    def act(self, out, in_, func, rd, wr, **kw):
        self.op("act", lambda e: e.activation(out=out, in_=in_, func=func, **kw), rd, wr)

    def tt(self, eng, out, a, b, op, rd, wr):
        if eng == "pool" and not POOL_TT:
            eng = "dve"
        self.op(eng, lambda e: e.tensor_tensor(out, a, b, op), rd, wr)

    def ts(self, eng, out, a, s1, s2, op0, op1, rd, wr):
        if s2 is None:
            self.op(eng, lambda e: e.tensor_scalar(out, a, s1, None, op0), rd, wr)
        else:
            self.op(eng, lambda e: e.tensor_scalar(out, a, s1, s2, op0, op1), rd, wr)

    def stt(self, eng, out, in0, scalar, in1, op0, op1, rd, wr):
        self.op(eng, lambda e: e.scalar_tensor_tensor(out, in0, scalar, in1, op0, op1), rd, wr)

    def cp(self, eng, out, in_, rd, wr):
        if eng == "act":
            self.op("act", lambda e: e.activation(out=out, in_=in_, func=AF.Copy), rd, wr)
        else:
            self.op(eng, lambda e: e.tensor_copy(out, in_), rd, wr)

    def red(self, eng, out, in_, rd, wr):
        self.op(eng, lambda e: e.tensor_reduce(out, in_, AX.X, ALU.add), rd, wr)

    def memset(self, eng, ap, val, wr):
        self.op(eng, lambda e: e.memset(ap, val), (), wr)


class Arena:
    def __init__(self, nc, kb, nbytes):
        self.t = nc.alloc_sbuf_tensor("arena", [128, nbytes // 2], BF16)
        self.off = 0
        self.cap = nbytes
        self.kb = kb

    def alloc(self, shape, dt, persist=False):
        n = 1
        for s in shape[1:]:
            n *= s
        esz = 4 if dt in (F32, I32) else 2
        b = (n * esz + 31) // 32 * 32
        if self.off + b > self.cap:
            raise RuntimeError("arena overflow: need %d at %d cap %d" % (b, self.off, self.cap))
        ap = self.t[:, self.off // 2:(self.off + b) // 2]
        self.off += b
        if dt != BF16:
            ap = ap.bitcast(dt)
        ap = ap[0:shape[0], 0:n]
        if len(shape) == 3:
            ap = ap.rearrange("p (a b) -> p a b", a=shape[1])
        elif len(shape) == 4:
            ap = ap.rearrange("p (a b c) -> p a b c", a=shape[1], b=shape[2])
        t = Tile(ap)
        if not persist:
            self.kb.track(t)
        return t


class Cfg:
    pass


def build(S, NS, NL, dbg=False):
    nc = bass.Bass("TRN2", target_bir_lowering=False)
    NT = S // 128
    c = Cfg()

    def din(name, shape, dt=F32):
        return nc.dram_tensor(name, list(shape), dt, kind="ExternalInput").ap()

    x = din("x", [NS, S, D])
    p = din("p", [NL, NS, S, 256])
    pos = din("positions", [NS, S], I32)
    attn_norm = din("attn_norm", [NL, D])
    w_in = din("w_in", [NL, D, IN_COLS])
    da_q_norm = din("da_q_norm", [NL, 64])
    da_k_norm = din("da_k_norm", [NL, 64])
    lq1 = din("da_lambda_q1", [NL, 64])
    lk1 = din("da_lambda_k1", [NL, 64])
    lq2 = din("da_lambda_q2", [NL, 64])
    lk2 = din("da_lambda_k2", [NL, 64])
    da_sub_norm = din("da_sub_norm", [NL, 128])
    ssd_conv_w = din("ssd_conv_w", [NL, 4, 1024])
    ssd_conv_b = din("ssd_conv_b", [NL, 1024])
    ssd_dt_bias = din("ssd_dt_bias", [NL, 8])
    ssd_a_log = din("ssd_a_log", [NL, 8])
    ssd_d = din("ssd_d", [NL, 8])
    ssd_norm = din("ssd_norm", [NL, 512])
    gla_w_gate2 = din("gla_w_gate2", [NL, 16, 256])
    gla_b_gate = din("gla_b_gate", [NL, 256])
    gla_norm = din("gla_norm", [NL, 128])
    w_out = din("w_out", [NL, 1536, D])
    ffn_norm = din("ffn_norm", [NL, D])
    w_ffn_gate = din("w_ffn_gate", [NL, D, DFF])
    w_ffn_up = din("w_ffn_up", [NL, D, DFF])
    w_ffn_down = din("w_ffn_down", [NL, DFF, D])
    ple_w_proj = din("ple_w_proj", [NL, 256, D])
    ple_w_gate = din("ple_w_gate", [NL, D, D])
    c_ident = din("c_ident", [128, 128], BF16)
    c_tri = din("c_tri", [128, 128], BF16)
    c_trif = din("c_trif", [128, 128])
    c_onesf = din("c_onesf", [128, 128])
    c_invf = din("c_invf", [128, 8])
    c_onesb = din("c_onesb", [128, 128], BF16)

    out = nc.dram_tensor("out", [NS, S, D], F32, kind="ExternalOutput").ap()
    okind = "ExternalOutput" if dbg else "Internal"
    PT = nc.dram_tensor("PT", [NS, S, 3584], BF16, kind=okind).ap()
    XT = nc.dram_tensor("XT", [NS, 1024, S], BF16, kind=okind).ap()
    LRT = nc.dram_tensor("LRT", [NS, 16, S], BF16, kind=okind).ap()
    DTS = nc.dram_tensor("DTS", [NS, S, 8], F32, kind=okind).ap()
    MIX = nc.dram_tensor("MIX", [NS, S, 1536], BF16, kind=okind).ap()

    kb = KB(nc)
    A = Arena(nc, kb, 207 * 1024)
    PS = []
    for i in range(8):
        t = Tile(nc.alloc_psum_tensor("ps%d" % i, [128, 512], F32)[:, :])
        t.bf = t.ap.bitcast(BF16)
        PS.append(t)

    ident = A.alloc([128, 128], BF16, persist=True)
    tri = A.alloc([128, 128], BF16, persist=True)
    trif = A.alloc([128, 128], F32, persist=True)
    onesf = A.alloc([128, 128], F32, persist=True)
    invf = A.alloc([128, 8], F32, persist=True)
    junk = A.alloc([128, 1024], BF16, persist=True)
    onesb = A.alloc([128, 128], BF16, persist=True)
    for t, src in ((ident, c_ident), (tri, c_tri), (trif, c_trif), (onesf, c_onesf), (invf, c_invf), (onesb, c_onesb)):
        kb.dma((t.ap, src), slot=t, wr=[t])
    kb.barrier()
    MARK = A.off
    cvt_i = [0]

    def load_w(dst, src, nk, ncols, stage, rows=128):
        CH = stage[0].ap.shape[1]
        for k in range(nk):
            for c0 in range(0, ncols, CH):
                n = min(CH, ncols - c0)
                st = stage[cvt_i[0] % len(stage)]
                eng = ("pool", "dve", "act")[cvt_i[0] % 3]
                cvt_i[0] += 1
                kb.dma((st.ap[0:rows, 0:n], src[k * rows:(k + 1) * rows, c0:c0 + n]), slot=st, wr=[st])
                kb.cp(eng, dst.ap[0:rows, k, c0:c0 + n], st.ap[0:rows, 0:n], rd=[st], wr=[])

    def bcast_load(t, src, eng="pool", mul=None):
        kb.dma((t.ap, src.partition_broadcast(128)), slot=t, wr=[t])
        if mul is not None:
            kb.ts(eng, t.ap, t.ap, float(mul), None, ALU.mult, None, rd=[], wr=[t])

    def rstd_from_ss(out_ap, ss_ap, n, rd, wr):
        kb.act(out_ap, ss_ap, AF.Ln, rd, wr, bias=float(n * EPS))
        kb.act(out_ap, out_ap, AF.Exp, (), wr, scale=-0.5)

    def split_hl(hi_ap, lo_ap, src_ap, rd, wr):
        kb.cp("dve", hi_ap, src_ap, rd, wr)
        kb.tt("dve", lo_ap, src_ap, hi_ap, ALU.subtract, rd, wr)

    ev_i = [0]

    def evac(out_ap, in_ap, rd, wr):
        eng = ("act", "dve")[ev_i[0] % 2]
        ev_i[0] += 1
        kb.cp(eng, out_ap, in_ap, rd, wr)


    def seq_driver(factories, skew=3, depth=2):
        active = []
        it = iter(factories)
        pending = next(it, None)
        while active or pending is not None:
            if pending is not None and (not active or (len(active) < depth and active[-1][1] >= skew)):
                active.append([pending(), 0])
                pending = next(it, None)
            for a in list(active):
                try:
                    next(a[0])
                    a[1] += 1
                except StopIteration:
                    active.remove(a)
            yield

    def phase_a1(L, hsrc):
        A.off = MARK
        Win = A.alloc([128, 8, IN_COLS], BF16)
        stage = [A.alloc([128, 1024], F32) for _ in range(3)]
        gain = A.alloc([128, D], F32)
        load_w(Win, w_in[L], 8, IN_COLS, stage)
        bcast_load(gain, attn_norm[L], mul=32.0)
        kb.barrier()
        hin = [A.alloc([128, D], F32) for _ in range(6)]
        abf = [A.alloc([128, D], BF16) for _ in range(2)]
        aT = [A.alloc([128, 8, 512], BF16) for _ in range(2)]
        ptm = [A.alloc([128, 3584], BF16) for _ in range(3)]
        dtsb = [A.alloc([128, 8], F32) for _ in range(3)]
        xts = [A.alloc([128, 8, 512], BF16) for _ in range(2)]
        lrs = [A.alloc([16, 512], BF16) for _ in range(2)]
        ss = [A.alloc([128, 2], F32) for _ in range(4)]
        tiles = [(s, t) for s in range(NS) for t in range(NT)]
        TM = [(0, 0, 512), (512, 512, 512), (1024, 1024, 512), (1536, 1536, 512),
              (3080, 2048, 512), (3592, 2560, 512), (4104, 3072, 512)]

        def load(g):
            s, t = tiles[g]
            h = hin[g % 6]
            kb.dma((h.ap, hsrc[s, t * 128:(t + 1) * 128, :]), slot=h, wr=[h])

        for g in range(min(5, len(tiles))):
            load(g)
        pi = 0
        for g, (s, t) in enumerate(tiles):
            if g + 5 < len(tiles):
                load(g + 5)
            j = t % 4
            u = g // 4
            h = hin[g % 6]
            sq = ss[g % 4]
            ab = abf[g % 2]
            at = aT[u % 2]
            kb.act(junk.ap, h.ap, AF.Square, rd=[h], wr=[junk, sq], accum_out=sq.ap[:, 0:1])
            rstd_from_ss(sq.ap[:, 1:2], sq.ap[:, 0:1], D, rd=[], wr=[sq])
            kb.stt("dve", ab.ap, h.ap, sq.ap[:, 1:2], gain.ap, ALU.mult, ALU.mult, rd=[h, sq, gain], wr=[ab])
            pst = PS[pi % 2]
            pi += 1
            for k in range(8):
                kb.tr(pst.bf[:, k * 128:(k + 1) * 128], ab.ap[:, k * 128:(k + 1) * 128], ident.ap, rd=[ab], wr=[pst])
            evac(at.ap[:, :, j * 128:(j + 1) * 128], pst.bf.rearrange("p (a b) -> p a b", a=8), rd=[pst], wr=[at])
            pt = ptm[g % 3]
            dt_ = dtsb[g % 3]
            for ci, (c0, d0, n) in enumerate(TM):
                ps = PS[2 + (ci % 4)]
                for k in range(8):
                    kb.mm(ps.ap[:, 0:n], at.ap[:, k, j * 128:(j + 1) * 128], Win.ap[:, k, c0:c0 + n],
                          k == 0, k == 7, rd=[at], wr=[ps])
                evac(pt.ap[:, d0:d0 + n], ps.ap[:, 0:n], rd=[ps], wr=[pt])
            ps = PS[6]
            for k in range(8):
                kb.mm(ps.ap[:, 0:8], at.ap[:, k, j * 128:(j + 1) * 128], Win.ap[:, k, 3072:3080], k == 0, k == 7,
                      rd=[at], wr=[ps])
            kb.cp("dve", dt_.ap, ps.ap[:, 0:8], rd=[ps], wr=[dt_])
            kb.dma((PT[s, t * 128:(t + 1) * 128, :], pt.ap), slot=pt, rd=[pt])
            kb.dma((DTS[s, t * 128:(t + 1) * 128, :], dt_.ap), slot=dt_, rd=[dt_])
            if j == 3:
                t0 = (t - 3) * 128
                xs = xts[u % 2]
                lr = lrs[u % 2]
                for b in range(8):
                    ps = PS[2 + (b % 4)]
                    for k in range(8):
                        kb.mm(ps.ap, Win.ap[:, k, 2048 + b * 128:2048 + (b + 1) * 128], at.ap[:, k, :], k == 0, k == 7,
                              rd=[at], wr=[ps])
                    evac(xs.ap[:, b, :], ps.ap, rd=[ps], wr=[xs])
                ps = PS[7]
                for k in range(8):
                    kb.mm(ps.ap[0:16, :], Win.ap[:, k, 4616:4632], at.ap[:, k, :], k == 0, k == 7, rd=[at], wr=[ps])
                evac(lr.ap, ps.ap[0:16, :], rd=[ps], wr=[lr])
                kb.dma((XT[s].rearrange("(b p) t -> p b t", p=128)[:, :, t0:t0 + 512], xs.ap), slot=xs, rd=[xs])
                kb.dma((LRT[s, :, t0:t0 + 512], lr.ap), slot=lr, rd=[lr])
        kb.barrier()

    def phase_da(L):
        lam_init = 0.8 - 0.6 * math.exp(-0.3 * L)
        for s in range(NS):
            A.off = MARK
            qT = A.alloc([128, 4, S], BF16)
            kT = A.alloc([128, 4, S], BF16)
            Vp = A.alloc([128, NT, 4, 130], BF16)
            gqk = A.alloc([128, 16, 64], F32)
            gsub = A.alloc([128, 128], F32)
            lamt = A.alloc([128, 4, 64], F32)
            lamv = A.alloc([128, 8], F32)
            posi = A.alloc([128, NT], I32)
            posf = A.alloc([128, NT], F32)
            cs = A.alloc([128, NT, 16], F32)
            for r in range(8):
                kb.dma((gqk.ap[:, r, :], da_q_norm[L].partition_broadcast(128)), slot=gqk, wr=[gqk])
                kb.dma((gqk.ap[:, 8 + r, :], da_k_norm[L].partition_broadcast(128)), slot=gqk, wr=[gqk])
            kb.ts("pool", gqk.ap, gqk.ap, 8.0, None, ALU.mult, None, rd=[], wr=[gqk])
            bcast_load(gsub, da_sub_norm[L], mul=math.sqrt(128.0) * (1.0 - lam_init))
            for i_, src in enumerate((lq1, lk1, lq2, lk2)):
                kb.dma((lamt.ap[:, i_, :], src[L].partition_broadcast(128)), slot=lamt, wr=[lamt])
            kb.tt("dve", lamt.ap[:, 0, :], lamt.ap[:, 0, :], lamt.ap[:, 1, :], ALU.mult, rd=[], wr=[lamt])
            kb.tt("dve", lamt.ap[:, 2, :], lamt.ap[:, 2, :], lamt.ap[:, 3, :], ALU.mult, rd=[], wr=[lamt])
            kb.red("dve", lamv.ap[:, 0:1], lamt.ap[:, 0, :], rd=[lamt], wr=[lamv])
            kb.red("dve", lamv.ap[:, 1:2], lamt.ap[:, 2, :], rd=[lamt], wr=[lamv])
            kb.act(lamv.ap[:, 2:4], lamv.ap[:, 0:2], AF.Exp, rd=[], wr=[lamv])
            kb.tt("dve", lamv.ap[:, 4:5], lamv.ap[:, 3:4], lamv.ap[:, 2:3], ALU.subtract, rd=[], wr=[lamv])
            kb.ts("dve", lamv.ap[:, 5:6], lamv.ap[:, 4:5], float(-lam_init), None, ALU.add, None, rd=[], wr=[lamv])
            kb.dma([(posi.ap[:, t_:t_ + 1], pos[s, t_ * 128:(t_ + 1) * 128].rearrange("(p o) -> p o", o=1)) for t_ in range(NT)],
                   slot=posi, wr=[posi])
            kb.cp("dve", posf.ap, posi.ap, rd=[posi], wr=[posf])
            kb.tt("dve", cs.ap[:, :, 8:16], posf.ap.unsqueeze(2).to_broadcast([128, NT, 8]),
                  invf.ap.unsqueeze(1).to_broadcast([128, NT, 8]), ALU.mult, rd=[posf], wr=[cs])
            kb.ts("dve", cs.ap[:, :, 0:8], cs.ap[:, :, 8:16], 0.5 * math.pi, None, ALU.add, None, rd=[], wr=[cs])
            ki_ = A.alloc([128, NT, 16], I32)
            kf_ = A.alloc([128, NT, 16], F32)
            kb.ts("dve", ki_.ap, cs.ap, 1.0 / (2 * math.pi), None, ALU.mult, None, rd=[cs], wr=[ki_])
            kb.cp("dve", kf_.ap, ki_.ap, rd=[ki_], wr=[kf_])
            kb.stt("dve", cs.ap, kf_.ap, -2 * math.pi, cs.ap, ALU.mult, ALU.add, rd=[kf_], wr=[cs])
            kb.ts("dve", kf_.ap, cs.ap, math.pi, None, ALU.is_gt, None, rd=[cs], wr=[kf_])
            kb.stt("dve", cs.ap, kf_.ap, -2 * math.pi, cs.ap, ALU.mult, ALU.add, rd=[kf_], wr=[cs])
            kb.ts("dve", kf_.ap, cs.ap, -math.pi, None, ALU.is_lt, None, rd=[cs], wr=[kf_])
            kb.stt("dve", cs.ap, kf_.ap, 2 * math.pi, cs.ap, ALU.mult, ALU.add, rd=[kf_], wr=[cs])
            kb.ts("dve", cs.ap, cs.ap, math.pi, -math.pi, ALU.min, ALU.max, rd=[], wr=[cs])
            kb.act(cs.ap, cs.ap, AF.Sin, rd=[], wr=[cs])
            kb.memset("pool", Vp.ap[:, :, :, 128:130], 1.0, wr=[Vp])
            if DA_STOP == 1:
                kb.barrier()
                continue
            mark_pro = A.off
            qk = [A.alloc([128, 16, 64], BF16) for _ in range(4)]
            sqb = [A.alloc([128, 16, 64], F32) for _ in range(2)]
            qn = [A.alloc([128, 16, 64], F32) for _ in range(2)]
            qb = [A.alloc([128, 16, 64], BF16) for _ in range(2)]
            st16 = [A.alloc([128, 32], F32) for _ in range(2)]
            rt = [A.alloc([128, 4, 16, 8], F32) for _ in range(2)]

            for t in range(NT):
                kb.dma((Vp.ap[:, t, :, 0:128], PT[s, t * 128:(t + 1) * 128, 1024:1536].rearrange("p (h d) -> p h d", h=4)),
                       slot=Vp, wr=[Vp])

            def pro_gen(t):
                q_, sq_, qn_, qb_, st_, rt_ = qk[t % 4], sqb[t % 2], qn[t % 2], qb[t % 2], st16[t % 2], rt[t % 2]
                kb.tt("dve", sq_.ap, q_.ap, q_.ap, ALU.mult, rd=[q_], wr=[sq_])
                yield
                kb.red("dve", st_.ap[:, 0:16], sq_.ap, rd=[sq_], wr=[st_])
                yield
                rstd_from_ss(st_.ap[:, 16:32], st_.ap[:, 0:16], 64, rd=[], wr=[st_])
                yield
                kb.tt("dve", qn_.ap, q_.ap, st_.ap[:, 16:32].unsqueeze(2).to_broadcast([128, 16, 64]), ALU.mult,
                      rd=[q_, st_], wr=[qn_])
                yield
                kb.tt("dve", qn_.ap, qn_.ap, gqk.ap, ALU.mult, rd=[gqk], wr=[qn_])
                yield
                kb.cp("act", qb_.ap, qn_.ap, rd=[qn_], wr=[qb_])
                cb_ = cs.ap[:, t, 0:8].unsqueeze(1).to_broadcast([128, 16, 8])
                sb_ = cs.ap[:, t, 8:16].unsqueeze(1).to_broadcast([128, 16, 8])
                x1 = qn_.ap[:, :, 0:8]
                x2 = qn_.ap[:, :, 8:16]
                kb.tt("dve", rt_.ap[:, 0], x1, cb_, ALU.mult, rd=[qn_, cs], wr=[rt_])
                kb.tt("dve", rt_.ap[:, 1], x2, sb_, ALU.mult, rd=[qn_, cs], wr=[rt_])
                kb.tt("dve", rt_.ap[:, 2], x2, cb_, ALU.mult, rd=[qn_, cs], wr=[rt_])
                kb.tt("dve", rt_.ap[:, 3], x1, sb_, ALU.mult, rd=[qn_, cs], wr=[rt_])
                yield
                kb.tt("dve", qb_.ap[:, :, 0:8], rt_.ap[:, 0], rt_.ap[:, 1], ALU.subtract, rd=[rt_], wr=[qb_])
                kb.tt("dve", qb_.ap[:, :, 8:16], rt_.ap[:, 2], rt_.ap[:, 3], ALU.add, rd=[rt_], wr=[qb_])
                yield
                pst = PS[t % 2]
                pst2 = PS[2 + t % 2]
                qbf = qb_.ap.rearrange("p a b -> p (a b)")
                for k in range(4):
                    kb.tr(pst.bf[:, k * 128:(k + 1) * 128], qbf[:, k * 128:(k + 1) * 128], ident.ap, rd=[qb_], wr=[pst])
                for k in range(4):
                    kb.tr(pst2.bf[:, k * 128:(k + 1) * 128], qbf[:, (4 + k) * 128:(5 + k) * 128], ident.ap, rd=[qb_], wr=[pst2])
                kb.cp("act", qT.ap[:, :, t * 128:(t + 1) * 128], pst.bf[:, 0:512].rearrange("p (a b) -> p a b", a=4), rd=[pst], wr=[qT])
                kb.cp("dve", kT.ap[:, :, t * 128:(t + 1) * 128], pst2.bf[:, 0:512].rearrange("p (a b) -> p a b", a=4), rd=[pst2], wr=[kT])

            def loadp(t):
                if t < NT:
                    q_ = qk[t % 4]
                    kb.dma((q_.ap.rearrange("p a b -> p (a b)"), PT[s, t * 128:(t + 1) * 128, 0:1024]), slot=q_, wr=[q_])

            loadp(0)
            loadp(1)
            for t in range(0, NT, 2):
                loadp(t + 2)
                loadp(t + 3)
                gens = [pro_gen(t), pro_gen(t + 1)]
                while gens:
                    for g_ in list(gens):
                        try:
                            next(g_)
                        except StopIteration:
                            gens.remove(g_)
            if DA_STOP == 2:
                kb.barrier()
                continue
            kb.barrier()
            A.off = mark_pro
            E = [A.alloc([128, 512], BF16) for _ in range(3)]
            osb = [A.alloc([128, 2, 4, 130], F32) for _ in range(2)]
            rr = [A.alloc([128, 16], F32) for _ in range(2)]
            o_ = [A.alloc([128, 4, 128], F32) for _ in range(2)]
            t2 = [A.alloc([128, 4, 128], F32) for _ in range(2)]
            yda = [A.alloc([128, 4, 4, 128], BF16) for _ in range(2)]
            it = 0
            blks = [(Q, h, c_, kt) for Q in range(S // 512) for h in range(4) for c_ in range(2) for kt in range(4 * Q + 4)]
            qk_next = [0]

            def emit_qk():
                n = qk_next[0]
                if n >= len(blks):
                    return
                qk_next[0] += 1
                Q, h, c_, kt = blks[n]
                i = kt - 4 * Q
                q0 = max(i, 0) * 128
                p0, p1 = c_ * 64, (c_ + 1) * 64
                psS = PS[n % 2]
                e_ = E[n % 3]
                kb.mm(psS.ap[:, q0:512], kT.ap[p0:p1, h, kt * 128:(kt + 1) * 128],
                      qT.ap[p0:p1, h, Q * 512 + q0:(Q + 1) * 512], True, True, rd=[kT, qT], wr=[psS])
                kb.act(e_.ap[:, q0:512], psS.ap[:, q0:512], AF.Exp, rd=[psS], wr=[e_], scale=0.125)
                if i >= 0:
                    kb.tt("dve", e_.ap[:, q0:q0 + 128], e_.ap[:, q0:q0 + 128], tri.ap, ALU.mult, rd=[], wr=[e_])

            gla_g = [phase_gla(L, fused_s=s) if FUSE_GLA else None]
            gla_every = max(1, (len(blks) * 2) // (NT * 7))

            def step_gla():
                if gla_g[0] is not None:
                    try:
                        next(gla_g[0])
                    except StopIteration:
                        gla_g[0] = None

            step_gla()
            emit_qk()
            n_blk = 0
            for Q in range(S // 512):
                yd = yda[Q % 2]
                for h in range(4):
                    ob = osb[it % 2]
                    for c_ in range(2):
                        accA = PS[2 + 2 * ((2 * it + c_) % 2)]
                        accB = PS[3 + 2 * ((2 * it + c_) % 2)]
                        accs = [accA.ap[:, 0:129], accA.ap[:, 130:259], accB.ap[:, 0:129], accB.ap[:, 130:259]]
                        acct = [accA, accA, accB, accB]
                        nkt = 4 * Q + 4
                        for kt in range(nkt):
                            assert blks[n_blk] == (Q, h, c_, kt)
                            e_ = E[n_blk % 3]
                            n_blk += 1
                            emit_qk()
                            if n_blk % gla_every == 0:
                                step_gla()
                            i = kt - 4 * Q
                            jmin = max(i, 0)
                            for j in range(jmin, 4):
                                last = (kt == 4 * Q + j)
                                kb.mm(accs[j], e_.ap[:, j * 128:(j + 1) * 128], Vp.ap[:, kt, h, 0:129], (kt == 0 and j % 2 == 0), last,
                                      rd=[e_, Vp], wr=[acct[j]], sig=(last or j == 3))
                        kb.cp("act", ob.ap[:, c_, 0:2, :].rearrange("p a b -> p (a b)"), accA.ap[:, 0:260], rd=[accA], wr=[ob])
                        kb.cp("dve", ob.ap[:, c_, 2:4, :].rearrange("p a b -> p (a b)"), accB.ap[:, 0:260], rd=[accB], wr=[ob])
                    r_ = rr[it % 2]
                    oo = o_[it % 2]
                    tt_ = t2[it % 2]
                    it += 1
                    kb.op("dve", lambda e, a=r_.ap[:, 0:8], b=ob.ap[:, :, :, 128]: e.reciprocal(a.rearrange("p (a b) -> p a b", a=2), b),
                          rd=[ob], wr=[r_])
                    kb.ts("dve", r_.ap[:, 4:8], r_.ap[:, 4:8], lamv.ap[:, 5:6], None, ALU.mult, None, rd=[lamv], wr=[r_])
                    kb.tt("dve", oo.ap, ob.ap[:, 0, :, 0:128], r_.ap[:, 0:4].unsqueeze(2).to_broadcast([128, 4, 128]), ALU.mult,
                          rd=[ob, r_], wr=[oo])
                    kb.tt("dve", tt_.ap, ob.ap[:, 1, :, 0:128], r_.ap[:, 4:8].unsqueeze(2).to_broadcast([128, 4, 128]), ALU.mult,
                          rd=[ob, r_], wr=[tt_])
                    kb.tt("dve", oo.ap, oo.ap, tt_.ap, ALU.add, rd=[tt_], wr=[oo])
                    kb.tt("dve", tt_.ap, oo.ap, oo.ap, ALU.mult, rd=[oo], wr=[tt_])
                    kb.red("dve", r_.ap[:, 8:12], tt_.ap, rd=[tt_], wr=[r_])
                    rstd_from_ss(r_.ap[:, 12:16], r_.ap[:, 8:12], 128, rd=[], wr=[r_])
                    kb.tt("dve", oo.ap, oo.ap, r_.ap[:, 12:16].unsqueeze(2).to_broadcast([128, 4, 128]), ALU.mult, rd=[r_], wr=[oo])
                    kb.tt("dve", yd.ap[:, :, h, :], oo.ap, gsub.ap.unsqueeze(1).to_broadcast([128, 4, 128]), ALU.mult,
                          rd=[oo, gsub], wr=[yd])
                kb.dma((MIX[s, Q * 512:(Q + 1) * 512, 0:512].rearrange("(j p) (h d) -> p j h d", p=128, h=4), yd.ap), slot=yd, rd=[yd])
            while gla_g[0] is not None:
                step_gla()
            kb.barrier()


    def phase_ssd(L):
        A.off = MARK

        def gen(s):
            cw = A.alloc([128, 8, 4], F32)
            cbias = A.alloc([128, 8], F32)
            diag = A.alloc([128, 8, 4, 128], BF16)
            dtb = A.alloc([128, 8], F32)
            abc = A.alloc([128, 8], F32)
            dsm = A.alloc([128, 8], F32)
            dbc = A.alloc([128, 8, 64], F32)
            gn = A.alloc([128, 512], F32)
            H = A.alloc([128, 8, 64], F32)
            prev = A.alloc([128, 512], BF16)
            cwv = ssd_conv_w[L].rearrange("j (b p) -> p b j", p=128)
            for b in range(8):
                kb.dma((cw.ap[:, b, :], cwv[:, b, :]), slot=cw, wr=[cw], slow=True)
            kb.dma([(cbias.ap[:, b_:b_ + 1], ssd_conv_b[L, b_ * 128:(b_ + 1) * 128].rearrange("(p o) -> p o", o=1)) for b_ in range(8)],
                   slot=cbias, wr=[cbias])
            bcast_load(dtb, ssd_dt_bias[L])
            bcast_load(abc, ssd_a_log[L])
            bcast_load(dsm, ssd_d[L])
            bcast_load(gn, ssd_norm[L], mul=16.0)
            kb.act(abc.ap, abc.ap, AF.Exp, rd=[], wr=[abc])
            kb.ts("dve", abc.ap, abc.ap, -1.0, None, ALU.mult, None, rd=[], wr=[abc])
            kb.cp("dve", dbc.ap, dsm.ap.unsqueeze(2).to_broadcast([128, 8, 64]), rd=[dsm], wr=[dbc])
            for b in range(8):
                for j in range(4):
                    kb.ts(("pool", "dve")[j % 2], diag.ap[:, b, j, :], ident.ap, cw.ap[:, b, j:j + 1], None, ALU.mult, None,
                          rd=[cw], wr=[diag])
            kb.memset("dve", H.ap, 0.0, wr=[H])
            kb.memset("pool", prev.ap, 0.0, wr=[prev])
            xTh = [A.alloc([128, 8, 516], BF16) for _ in range(2)]
            xa = [A.alloc([128, 8, 512], BF16) for _ in range(2)]
            zt = [A.alloc([128, 512], BF16) for _ in range(3)]
            dtt = [A.alloc([128, 8], F32) for _ in range(3)]
            xbt = [A.alloc([128, 768], BF16) for _ in range(2)]
            smt = [A.alloc([128, 80], F32) for _ in range(2)]
            dabt = [A.alloc([128, 2, 8, 128], BF16) for _ in range(2)]
            hlt = [A.alloc([128, 16], BF16) for _ in range(2)]
            segt = [A.alloc([128, 8, 128], F32) for _ in range(2)]
            cbmt = [A.alloc([128, 2, 128], F32) for _ in range(2)]
            mtt = [A.alloc([128, 8, 128], BF16) for _ in range(2)]
            xdtt = [A.alloc([128, 8, 64], BF16) for _ in range(2)]
            xdd = [A.alloc([128, 8, 64], BF16) for _ in range(2)]
            xdwt = [A.alloc([128, 8, 64], BF16) for _ in range(2)]
            yt = [A.alloc([128, 512], F32) for _ in range(2)]
            szt = [A.alloc([128, 512], F32) for _ in range(2)]
            nrt = [A.alloc([128, 8], F32) for _ in range(2)]
            yot = [A.alloc([128, 512], BF16) for _ in range(3)]
            XTv = XT[s].rearrange("(b p) t -> p b t", p=128)
            NU = S // 512

            def load_x(U):
                xh = xTh[U % 2]
                if U == 0:
                    kb.memset("pool", xh.ap[:, :, 0:3], 0.0, wr=[xh])
                    kb.dma((xh.ap[:, :, 3:515], XTv[:, :, 0:512]), slot=xh, wr=[xh])
                else:
                    kb.dma((xh.ap[:, :, 0:515], XTv[:, :, U * 512 - 3:U * 512 + 512]), slot=xh, wr=[xh])

            def load_c(t):
                kb.dma((zt[t % 3].ap, PT[s, t * 128:(t + 1) * 128, 1536:2048]), slot=zt[t % 3], wr=[zt[t % 3]])
                kb.dma((dtt[t % 3].ap, DTS[s, t * 128:(t + 1) * 128, :]), slot=dtt[t % 3], wr=[dtt[t % 3]])

            load_x(0)
            load_c(0)
            done = [0]

            def conv_stage(U):
                if U + 1 < NU:
                    load_x(U + 1)
                xh = xTh[U % 2]
                xa_ = xa[U % 2]
                for b in range(8):
                    ps = PS[b % 2]
                    for j in range(4):
                        kb.mm(ps.ap, diag.ap[:, b, j, :], xh.ap[:, b, j:j + 512], j == 0, j == 3, rd=[diag, xh], wr=[ps])
                    kb.act(xa_.ap[:, b, :], ps.ap, AF.Silu, rd=[ps, cbias], wr=[xa_], bias=cbias.ap[:, b:b + 1])

            def chunk_gen(U, cc_only):
                xh = xTh[U % 2]
                xa_ = xa[U % 2]
                if cc_only == 0:
                    conv_stage(U)
                    yield
                for cc in (cc_only,):
                    t = U * 4 + cc
                    cs_ = cc * 128
                    if t + 1 < NT:
                        load_c(t + 1)
                    z, dtr, xb, sm = zt[t % 3], dtt[t % 3], xbt[t % 2], smt[t % 2]
                    dab, seg, cbm, MT = dabt[t % 2], segt[t % 2], cbmt[t % 2], mtt[t % 2]
                    xdt, xd, xdw, y, sz, nr, yo = xdtt[t % 2], xdd[t % 2], xdwt[t % 2], yt[t % 2], szt[t % 2], nrt[t % 2], yot[t % 3]
                    S_ = sm.ap
                    pst = PS[2]
                    for b in range(6):
                        kb.tr(pst.bf[:, b * 128:(b + 1) * 128], xa_.ap[:, b, cs_:cs_ + 128], ident.ap, rd=[xa_], wr=[pst])
                    kb.cp("act", xb.ap, pst.bf[:, 0:768], rd=[pst], wr=[xb])
                    yield
                    kb.tt("dve", S_[:, 0:8], dtr.ap, dtb.ap, ALU.add, rd=[dtr, dtb], wr=[sm])
                    kb.act(S_[:, 0:8], S_[:, 0:8], AF.Exp, rd=[], wr=[sm])
                    kb.act(S_[:, 0:8], S_[:, 0:8], AF.Ln, rd=[], wr=[sm], bias=1.0)
                    kb.tt("dve", S_[:, 8:16], S_[:, 0:8], abc.ap, ALU.mult, rd=[abc], wr=[sm])
                    pss = PS[3]
                    hl = hlt[t % 2]
                    split_hl(hl.ap[:, 0:8], hl.ap[:, 8:16], S_[:, 8:16], rd=[sm], wr=[hl])
                    kb.mm(pss.ap[:, 0:8], tri.ap, hl.ap[:, 0:8], True, False, rd=[hl], wr=[pss], sig=False)
                    kb.mm(pss.ap[:, 0:8], tri.ap, hl.ap[:, 8:16], False, True, rd=[hl], wr=[pss], sig=False)
                    kb.mm(pss.ap[:, 8:16], onesb.ap, hl.ap[:, 0:8], False, False, rd=[hl], wr=[pss], sig=False)
                    kb.mm(pss.ap[:, 8:16], onesb.ap, hl.ap[:, 8:16], False, True, rd=[hl], wr=[pss])
                    kb.cp("dve", S_[:, 16:32], pss.ap[:, 0:16], rd=[pss], wr=[sm])
                    yield
                    kb.cp("dve", S_[:, 32:40], S_[:, 16:24], rd=[], wr=[sm])
                    kb.tt("dve", S_[:, 40:48], S_[:, 24:32], S_[:, 16:24], ALU.subtract, rd=[], wr=[sm])
                    kb.cp("dve", S_[:, 48:56], S_[:, 24:32], rd=[], wr=[sm])
                    kb.act(S_[:, 56:80], S_[:, 32:56], AF.Exp, rd=[], wr=[sm])
                    kb.cp("dve", dab.ap[:, 0], hl.ap[:, 0:8].unsqueeze(2).to_broadcast([128, 8, 128]), rd=[hl], wr=[dab])
                    kb.cp("dve", dab.ap[:, 1], hl.ap[:, 8:16].unsqueeze(2).to_broadcast([128, 8, 128]), rd=[hl], wr=[dab])
                    psc = (PS[4], PS[5])
                    for h in range(8):
                        cols = psc[h // 4].ap[:, (h % 4) * 128:(h % 4 + 1) * 128]
                        kb.mm(cols, dab.ap[:, 0, h, :], tri.ap, (h % 4 == 0), False, rd=[dab], wr=[psc[h // 4]], sig=False)
                        kb.mm(cols, dab.ap[:, 1, h, :], tri.ap, False, True, rd=[dab], wr=[psc[h // 4]])
                    for h in range(8):
                        kb.ts("dve", seg.ap[:, h, :], psc[h // 4].ap[:, (h % 4) * 128:(h % 4 + 1) * 128], S_[:, 16 + h:17 + h], 0.0,
                              ALU.subtract, ALU.min, rd=[psc[h // 4], sm], wr=[seg])
                    yield
                    kb.act(seg.ap, seg.ap, AF.Exp, rd=[], wr=[seg])
                    psb = PS[6]
                    for g in range(2):
                        kb.mm(psb.ap[:, g * 128:(g + 1) * 128], xa_.ap[:, 4 + g, cs_:cs_ + 128], xa_.ap[:, 6 + g, cs_:cs_ + 128],
                              True, True, rd=[xa_], wr=[psb])
                    kb.tt("dve", cbm.ap, psb.ap[:, 0:256].rearrange("p (g l) -> p g l", g=2),
                          trif.ap.unsqueeze(1).to_broadcast([128, 2, 128]), ALU.mult, rd=[psb, trif], wr=[cbm])
                    yield
                    while done[0] < t:
                        yield
                    kb.tt("dve", MT.ap.rearrange("p (g r) l -> p g r l", g=2), seg.ap.rearrange("p (g r) l -> p g r l", g=2),
                          cbm.ap.unsqueeze(2).to_broadcast([128, 2, 4, 128]), ALU.mult, rd=[seg, cbm], wr=[MT])
                    xv = xb.ap[:, 0:512].rearrange("p (h d) -> p h d", h=8)
                    kb.tt("dve", xdt.ap, xv, S_[:, 0:8].unsqueeze(2).to_broadcast([128, 8, 64]), ALU.mult, rd=[xb, sm], wr=[xdt])
                    kb.tt("pool", xd.ap, xv, dbc.ap, ALU.mult, rd=[xb, dbc], wr=[xd])
                    kb.tt("pool", xdw.ap, xdt.ap, S_[:, 64:72].unsqueeze(2).to_broadcast([128, 8, 64]), ALU.mult, rd=[xdt, sm], wr=[xdw])
                    psy = PS[7]
                    kb.mm(psy.ap, ident.ap, xd.ap.rearrange("p h d -> p (h d)"), True, False, rd=[xd], wr=[psy], sig=False)
                    for h in range(8):
                        kb.mm(psy.ap[:, h * 64:(h + 1) * 64], MT.ap[:, h, :], xdt.ap[:, h, :], False, h == 7, rd=[MT, xdt], wr=[psy])
                    psyo = PS[0]
                    for g in range(2):
                        kb.mm(psyo.ap[:, g * 256:(g + 1) * 256], xa_.ap[:, 6 + g, cs_:cs_ + 128], prev.ap[:, g * 256:(g + 1) * 256],
                              True, True, rd=[xa_, prev], wr=[psyo])
                    kb.tt("dve", y.ap.rearrange("p (h d) -> p h d", h=8), psyo.ap.rearrange("p (h d) -> p h d", h=8),
                          S_[:, 56:64].unsqueeze(2).to_broadcast([128, 8, 64]), ALU.mult, rd=[psyo, sm], wr=[y])
                    kb.tt("dve", y.ap, y.ap, psy.ap, ALU.add, rd=[psy], wr=[y])
                    yield
                    kb.act(sz.ap, z.ap, AF.Silu, rd=[z], wr=[sz])
                    kb.tt("pool", y.ap, y.ap, sz.ap, ALU.mult, rd=[sz], wr=[y])
                    for g in range(2):
                        kb.act(junk.ap[:, 0:256], y.ap[:, g * 256:(g + 1) * 256], AF.Square, rd=[y], wr=[junk, nr],
                               accum_out=nr.ap[:, g:g + 1])
                    rstd_from_ss(nr.ap[:, 2:4], nr.ap[:, 0:2], 256, rd=[], wr=[nr])
                    for g in range(2):
                        kb.stt("dve", yo.ap[:, g * 256:(g + 1) * 256], y.ap[:, g * 256:(g + 1) * 256], nr.ap[:, 2 + g:3 + g],
                               gn.ap[:, g * 256:(g + 1) * 256], ALU.mult, ALU.mult, rd=[y, nr, gn], wr=[yo])
                    kb.dma((MIX[s, t * 128:(t + 1) * 128, 512:1024], yo.ap), slot=yo, rd=[yo])
                    psst = PS[1]
                    xdwf = xdw.ap.rearrange("p h d -> p (h d)")
                    for g in range(2):
                        kb.mm(psst.ap[:, g * 256:(g + 1) * 256], xb.ap[:, 512 + g * 128:512 + (g + 1) * 128],
                              xdwf[:, g * 256:(g + 1) * 256], True, True, rd=[xb, xdw], wr=[psst])
                    kb.tt("dve", H.ap, H.ap, S_[:, 72:80].unsqueeze(2).to_broadcast([128, 8, 64]), ALU.mult, rd=[sm], wr=[H])
                    Hf = H.ap.rearrange("p h d -> p (h d)")
                    kb.tt("dve", Hf, Hf, psst.ap, ALU.add, rd=[psst], wr=[H])
                    kb.cp("act", prev.ap, Hf, rd=[H], wr=[prev])
                    done[0] += 1

            yield from seq_driver([(lambda U=U, cc=cc: chunk_gen(U, cc)) for U in range(NU) for cc in range(4)])

        gens = [[gen(s_), SEQ_SKEW * s_] for s_ in range(NS)]
        while gens:
            for g_ in list(gens):
                if g_[1] > 0:
                    g_[1] -= 1
                    continue
                try:
                    next(g_[0])
                except StopIteration:
                    gens.remove(g_)
        kb.barrier()

    def phase_gla(L, fused_s=None):
        if fused_s is None:
            if FUSE_GLA:
                return None
            A.off = MARK
        B = PS if fused_s is None else [PS[6], PS[6], PS[7], PS[6], PS[7], PS[6], PS[7], PS[6]]

        def gen(s):
            wgf = A.alloc([32, 256], F32)
            wg = A.alloc([32, 256], BF16)
            gnb = A.alloc([128, 128], F32)
            P = A.alloc([128, 2, 128], F32)
            prevb = A.alloc([128, 4, 128], BF16)
            kb.memset("dve", wgf.ap, 0.0, wr=[wgf])
            kb.dma((wgf.ap[0:16, :], gla_w_gate2[L]), slot=wgf, wr=[wgf])
            kb.dma((wgf.ap[16:17, :], gla_b_gate[L].rearrange("(o n) -> o n", o=1)), slot=wgf, wr=[wgf])
            kb.cp("dve", wg.ap, wgf.ap, rd=[wgf], wr=[wg])
            bcast_load(gnb, gla_norm[L], mul=math.sqrt(128.0))
            kb.memset("dve", P.ap, 0.0, wr=[P])
            kb.memset("pool", prevb.ap, 0.0, wr=[prevb])
            lrt = [A.alloc([32, 128], BF16) for _ in range(3)]
            for l_ in lrt:
                kb.memset("pool", l_.ap, 1.0, wr=[l_])
            gin = [A.alloc([128, 1536], BF16) for _ in range(3)]
            ut = [A.alloc([128, 256], F32) for _ in range(2)]
            uht = [A.alloc([128, 2, 256], BF16) for _ in range(2)]
            eet = [A.alloc([128, 3, 256], F32) for _ in range(2)]
            qdt = [A.alloc([128, 256], BF16) for _ in range(2)]
            kit = [A.alloc([128, 256], BF16) for _ in range(2)]
            ket = [A.alloc([128, 256], BF16) for _ in range(2)]
            qkTt = [A.alloc([128, 4, 128], BF16) for _ in range(2)]
            attmt = [A.alloc([128, 4, 128], BF16) for _ in range(2)]
            kipt = [A.alloc([128, 4, 128], BF16) for _ in range(2)]
            for k_ in kipt:
                kb.memset("dve", k_.ap, 0.0, wr=[k_])
            ot = [A.alloc([128, 4, 128], F32) for _ in range(2)]
            sqt = [A.alloc([128, 4, 128], F32) for _ in range(2)]
            nrt = [A.alloc([128, 8], F32) for _ in range(2)]
            sgt = [A.alloc([128, 512], F32) for _ in range(2)]
            ygt = [A.alloc([128, 512], BF16) for _ in range(3)]
            cdt = [A.alloc([128, 2], F32) for _ in range(2)]

            def load(t):
                kb.dma((gin[t % 3].ap, PT[s, t * 128:(t + 1) * 128, 2048:3584]), slot=gin[t % 3], wr=[gin[t % 3]])
                kb.dma((lrt[t % 3].ap[0:16, :], LRT[s, :, t * 128:(t + 1) * 128]), slot=lrt[t % 3], wr=[lrt[t % 3]])

            load(0)
            done = [0]

            def tile_gen(t_only):
              for t in (t_only,):
                if t + 1 < NT:
                    load(t + 1)
                g_, lr, u, ee, qd, ki, ke = gin[t % 3], lrt[t % 3], ut[t % 2], eet[t % 2], qdt[t % 2], kit[t % 2], ket[t % 2]
                qkT, attm, o, sq_, nr, sg, yg, cd = qkTt[t % 2], attmt[t % 2], ot[t % 2], sqt[t % 2], nrt[t % 2], sgt[t % 2], ygt[t % 3], cdt[t % 2]
                if GLA_CUT <= 1:
                    continue
                psx = B[0]
                kb.mm(psx.ap[:, 0:256], lr.ap[0:32, :], wg.ap[0:32, :], True, True, rd=[lr, wg], wr=[psx])
                kb.act(u.ap, psx.ap[:, 0:256], AF.Exp, rd=[psx], wr=[u], scale=-1.0)
                kb.act(u.ap, u.ap, AF.Ln, rd=[], wr=[u], bias=1.0)
                yield
                if GLA_CUT <= 2:
                    continue
                psc, pstot = B[1], B[2]
                uh = uht[t % 2]
                split_hl(uh.ap[:, 0, :], uh.ap[:, 1, :], u.ap, rd=[u], wr=[uh])
                kb.mm(psc.ap[:, 0:256], tri.ap, uh.ap[:, 0, :], True, False, rd=[uh], wr=[psc], sig=False)
                kb.mm(psc.ap[:, 0:256], tri.ap, uh.ap[:, 1, :], False, True, rd=[uh], wr=[psc])
                kb.mm(pstot.ap[:, 0:256], onesb.ap, uh.ap[:, 0, :], True, False, rd=[uh], wr=[pstot], sig=False)
                kb.mm(pstot.ap[:, 0:256], onesb.ap, uh.ap[:, 1, :], False, True, rd=[uh], wr=[pstot])
                csb = sgt[t % 2]
                kb.cp("act", csb.ap[:, 0:256], psc.ap[:, 0:256], rd=[psc], wr=[csb])
                kb.act(ee.ap[:, 0, :], csb.ap[:, 0:256], AF.Exp, rd=[csb], wr=[ee], scale=-1.0 / 16)
                kb.act(ee.ap[:, 1, :], csb.ap[:, 0:256], AF.Exp, rd=[csb], wr=[ee], scale=1.0 / 16)
                kb.tt("dve", ee.ap[:, 2, :], pstot.ap[:, 0:256], csb.ap[:, 0:256], ALU.subtract, rd=[pstot, csb], wr=[ee])
                kb.act(ee.ap[:, 2, :], ee.ap[:, 2, :], AF.Exp, rd=[], wr=[ee], scale=-1.0 / 16)
                yield
                if GLA_CUT <= 3:
                    continue
                kb.stt("dve", qd.ap, g_.ap[:, 0:256], 0.125, ee.ap[:, 0, :], ALU.mult, ALU.mult, rd=[g_, ee], wr=[qd])
                kb.tt("pool", ki.ap, g_.ap[:, 256:512], ee.ap[:, 1, :], ALU.mult, rd=[g_, ee], wr=[ki])
                kb.tt("pool", ke.ap, g_.ap[:, 256:512], ee.ap[:, 2, :], ALU.mult, rd=[g_, ee], wr=[ke])
                pst = B[3]
                for j in range(2):
                    kb.tr(pst.bf[:, j * 128:(j + 1) * 128], qd.ap[:, j * 128:(j + 1) * 128], ident.ap, rd=[qd], wr=[pst])
                for j in range(2):
                    kb.tr(pst.bf[:, (2 + j) * 128:(3 + j) * 128], ki.ap[:, j * 128:(j + 1) * 128], ident.ap, rd=[ki], wr=[pst])
                kb.cp("act", qkT.ap[:, 0:2, :].rearrange("p a b -> p (a b)"), pst.bf[:, 0:256], rd=[pst], wr=[qkT])
                kip = kipt[t % 2]
                for j in range(2):
                    for i in range(2):
                        kb.cp("act", kip.ap[i * 64:(i + 1) * 64, 2 * j + i, :], pst.bf[i * 64:(i + 1) * 64, (2 + j) * 128:(3 + j) * 128],
                              rd=[pst], wr=[kip])
                yield
                if GLA_CUT <= 4:
                    continue
                psatt = B[4]
                for h in range(4):
                    j, i = h // 2, h % 2
                    kb.mm(psatt.ap[:, h * 128:(h + 1) * 128], kip.ap[:, h, :], qkT.ap[:, j, :],
                          True, True, rd=[qkT, kip], wr=[psatt])
                if GLA_SUB <= 1:
                    continue
                kb.tt("dve", attm.ap, psatt.ap.rearrange("p (h l) -> p h l", h=4), trif.ap.unsqueeze(1).to_broadcast([128, 4, 128]),
                      ALU.mult, rd=[psatt, trif], wr=[attm])
                yield
                while done[0] < t:
                    yield
                if GLA_SUB <= 2:
                    continue
                pso = B[5]
                for h in range(4):
                    j, i = h // 2, h % 2
                    kb.mm(pso.ap[:, h * 128:(h + 1) * 128], attm.ap[:, h, :], g_.ap[:, 512 + h * 128:512 + (h + 1) * 128], True, False,
                          rd=[attm, g_], wr=[pso], sig=False)
                    kb.mm(pso.ap[:, h * 128:(h + 1) * 128], qkT.ap[:, j, :], prevb.ap[:, h, :],
                          False, True, rd=[qkT, prevb], wr=[pso])
                if GLA_SUB <= 3:
                    continue
                of = o.ap.rearrange("p a b -> p (a b)")
                kb.cp("act", of, pso.ap, rd=[pso], wr=[o])
                yield
                if GLA_CUT <= 5:
                    continue
                kb.tt("pool", sq_.ap, o.ap, o.ap, ALU.mult, rd=[o], wr=[sq_])
                kb.red("dve", nr.ap[:, 0:4], sq_.ap, rd=[sq_], wr=[nr])
                rstd_from_ss(nr.ap[:, 4:8], nr.ap[:, 0:4], 128, rd=[], wr=[nr])
                kb.act(sg.ap, g_.ap[:, 1024:1536], AF.Silu, rd=[g_], wr=[sg])
                kb.tt("dve", o.ap, o.ap, nr.ap[:, 4:8].unsqueeze(2).to_broadcast([128, 4, 128]), ALU.mult, rd=[nr], wr=[o])
                kb.tt("pool", o.ap, o.ap, gnb.ap.unsqueeze(1).to_broadcast([128, 4, 128]), ALU.mult, rd=[gnb], wr=[o])
                kb.tt("dve", yg.ap, of, sg.ap, ALU.mult, rd=[o, sg], wr=[yg])
                kb.dma((MIX[s, t * 128:(t + 1) * 128, 1024:1536], yg.ap), slot=yg, rd=[yg])
                yield
                if GLA_CUT <= 6:
                    continue
                pss_ = B[6]
                for j in range(2):
                    kb.mm(pss_.ap[:, j * 256:(j + 1) * 256], ke.ap[:, j * 128:(j + 1) * 128],
                          g_.ap[:, 512 + j * 256:512 + (j + 1) * 256], True, True, rd=[ke, g_], wr=[pss_])
                pscd = B[7]
                for j in range(2):
                    kb.mm(pscd.ap[:, j * 8:j * 8 + 8], uh.ap[:, 0, j * 128:(j + 1) * 128], onesb.ap[:, 0:8], (j == 0), False, rd=[uh], wr=[pscd], sig=False)
                    kb.mm(pscd.ap[:, j * 8:j * 8 + 8], uh.ap[:, 1, j * 128:(j + 1) * 128], onesb.ap[:, 0:8], False, True, rd=[uh], wr=[pscd])
                kb.act(cd.ap, pscd.ap[:, 0:16].rearrange("p (j e) -> p j e", j=2)[:, :, 0], AF.Exp, rd=[pscd], wr=[cd], scale=-1.0 / 16)
                for j in range(2):
                    for i in range(2):
                        kb.stt("dve", P.ap[i * 64:(i + 1) * 64, j, :], P.ap[i * 64:(i + 1) * 64, j, :], cd.ap[i * 64:(i + 1) * 64, j:j + 1],
                               pss_.ap[i * 64:(i + 1) * 64, j * 256 + i * 128:j * 256 + (i + 1) * 128], ALU.mult, ALU.add,
                               rd=[cd, pss_], wr=[P])
                for j in range(2):
                    for i in range(2):
                        kb.cp("act", prevb.ap[i * 64:(i + 1) * 64, 2 * j + i, :], P.ap[i * 64:(i + 1) * 64, j, :], rd=[P], wr=[prevb])
                done[0] += 1

            yield from seq_driver([(lambda t=t: tile_gen(t)) for t in range(NT)])

        if fused_s is not None:
            return gen(fused_s)
        gens = [[gen(s_), SEQ_SKEW * s_] for s_ in range(NS)]
        while gens:
            for g_ in list(gens):
                if g_[1] > 0:
                    g_[1] -= 1
                    continue
                try:
                    next(g_[0])
                except StopIteration:
                    gens.remove(g_)
        kb.barrier()

    def phase_a3(L, hsrc):
        A.off = MARK
        Wo = A.alloc([128, 12, D], BF16)
        stage = [A.alloc([128, 1024], F32) for _ in range(3)]
        load_w(Wo, w_out[L], 12, D, stage)
        kb.barrier()
        mixt = [A.alloc([128, 1536], BF16) for _ in range(4)]
        hin = [A.alloc([128, D], F32) for _ in range(4)]
        mT = [A.alloc([128, 12, 128], BF16) for _ in range(2)]
        h1 = [A.alloc([128, D], F32) for _ in range(3)]
        tiles = [(s, t) for s in range(NS) for t in range(NT)]

        def load(g):
            s, t = tiles[g]
            kb.dma((mixt[g % 4].ap, MIX[s, t * 128:(t + 1) * 128, :]), slot=mixt[g % 4], wr=[mixt[g % 4]])
            kb.dma((hin[g % 4].ap, hsrc[s, t * 128:(t + 1) * 128, :]), slot=hin[g % 4], wr=[hin[g % 4]])

        load(0)
        load(1)
        for g, (s, t) in enumerate(tiles):
            if g + 2 < len(tiles):
                load(g + 2)
            mx, h, m_, ho = mixt[g % 4], hin[g % 4], mT[g % 2], h1[g % 3]
            pa, pb = PS[0], PS[1]
            for k in range(12):
                dst = pa.bf[:, k * 128:(k + 1) * 128] if k < 8 else pb.bf[:, (k - 8) * 128:(k - 7) * 128]
                kb.tr(dst, mx.ap[:, k * 128:(k + 1) * 128], ident.ap, rd=[mx], wr=[pa if k < 8 else pb])
            kb.cp("act", m_.ap[:, 0:8, :], pa.bf.rearrange("p (a b) -> p a b", a=8), rd=[pa], wr=[m_])
            kb.cp("dve", m_.ap[:, 8:12, :], pb.bf[:, 0:512].rearrange("p (a b) -> p a b", a=4), rd=[pb], wr=[m_])
            for n in range(2):
                ps = PS[2 + (2 * g + n) % 4]
                for k in range(12):
                    kb.mm(ps.ap, m_.ap[:, k, :], Wo.ap[:, k, n * 512:(n + 1) * 512], k == 0, k == 11, rd=[m_], wr=[ps])
                kb.tt("dve", ho.ap[:, n * 512:(n + 1) * 512], ps.ap, h.ap[:, n * 512:(n + 1) * 512], ALU.add, rd=[ps, h], wr=[ho])
            kb.dma((out[s, t * 128:(t + 1) * 128, :], ho.ap), slot=ho, rd=[ho])
        kb.barrier()

    def phase_ffn(L):
        A.off = MARK
        Wg = A.alloc([128, 8, DFF], BF16)
        Wu = A.alloc([128, 8, DFF], BF16)
        Wd = A.alloc([128, 22, D], BF16)
        gain = A.alloc([128, D], F32)
        mark2 = A.off
        stage = [A.alloc([128, 1024], F32) for _ in range(2)]
        load_w(Wg, w_ffn_gate[L], 8, DFF, stage)
        load_w(Wu, w_ffn_up[L], 8, DFF, stage)
        load_w(Wd, w_ffn_down[L], 22, D, stage)
        bcast_load(gain, ffn_norm[L], mul=32.0)
        kb.barrier()
        A.off = mark2
        NJ = 4
        hin = [A.alloc([128, D], F32) for _ in range(4)]
        fb = [A.alloc([128, D], BF16) for _ in range(2)]
        fT = [A.alloc([128, 8, 128 * NJ], BF16) for _ in range(2)]
        gT = A.alloc([128, 22, 128 * NJ], BF16)
        sgb = [A.alloc([128, 128 * NJ], BF16) for _ in range(2)]
        ss = [A.alloc([128, 2], F32) for _ in range(4)]
        tiles = [(s, t) for s in range(NS) for t in range(NT)]
        W_ = 128 * NJ

        def load(g):
            s, t = tiles[g]
            kb.dma((hin[g % 4].ap, out[s, t * 128:(t + 1) * 128, :]), slot=hin[g % 4], wr=[hin[g % 4]])

        for g in range(4):
            load(g)
        for u in range(len(tiles) // NJ):
            ft = fT[u % 2]
            for jj in range(NJ):
                g = NJ * u + jj
                h, sq, f_ = hin[g % 4], ss[g % 4], fb[g % 2]
                kb.act(junk.ap, h.ap, AF.Square, rd=[h], wr=[junk, sq], accum_out=sq.ap[:, 0:1])
                rstd_from_ss(sq.ap[:, 1:2], sq.ap[:, 0:1], D, rd=[], wr=[sq])
                kb.stt("dve", f_.ap, h.ap, sq.ap[:, 1:2], gain.ap, ALU.mult, ALU.mult, rd=[h, sq, gain], wr=[f_])
                pst = PS[g % 2]
                for k in range(8):
                    kb.tr(pst.bf[:, k * 128:(k + 1) * 128], f_.ap[:, k * 128:(k + 1) * 128], ident.ap, rd=[f_], wr=[pst])
                evac(ft.ap[:, :, jj * 128:(jj + 1) * 128], pst.bf.rearrange("p (a b) -> p a b", a=8), rd=[pst], wr=[ft])
            for hb in range(22):
                psg, psu = PS[2 + (hb % 2) * 2], PS[3 + (hb % 2) * 2]
                for k in range(8):
                    kb.mm(psg.ap[:, 0:W_], Wg.ap[:, k, hb * 128:(hb + 1) * 128], ft.ap[:, k, :], k == 0, k == 7, rd=[ft], wr=[psg])
                for k in range(8):
                    kb.mm(psu.ap[:, 0:W_], Wu.ap[:, k, hb * 128:(hb + 1) * 128], ft.ap[:, k, :], k == 0, k == 7, rd=[ft], wr=[psu])
                sg_ = sgb[hb % 2]
                kb.act(sg_.ap, psg.ap[:, 0:W_], AF.Silu, rd=[psg], wr=[sg_])
                kb.tt("dve", gT.ap[:, hb, :], sg_.ap, psu.ap[:, 0:W_], ALU.mult, rd=[sg_, psu], wr=[gT])
            for jj in range(NJ):
                g = NJ * u + jj
                s, t = tiles[g]
                h = hin[g % 4]
                for n in range(2):
                    ps = PS[6 + n]
                    for k2 in range(22):
                        kb.mm(ps.ap, gT.ap[:, k2, jj * 128:(jj + 1) * 128], Wd.ap[:, k2, n * 512:(n + 1) * 512], k2 == 0, k2 == 21,
                              rd=[gT], wr=[ps])
                    kb.tt("dve", h.ap[:, n * 512:(n + 1) * 512], ps.ap, h.ap[:, n * 512:(n + 1) * 512], ALU.add, rd=[ps], wr=[h])
                kb.dma((out[s, t * 128:(t + 1) * 128, :], h.ap), slot=h, rd=[h])
                if g + 4 < len(tiles):
                    load(g + 4)
        kb.barrier()

    def phase_ple(L):
        A.off = MARK
        Wpg = A.alloc([128, 8, D], BF16)
        Wpp = A.alloc([128, 2, D], BF16)
        stage = [A.alloc([128, 1024], F32) for _ in range(3)]
        load_w(Wpg, ple_w_gate[L], 8, D, stage)
        load_w(Wpp, ple_w_proj[L], 2, D, stage)
        kb.barrier()
        hin = [A.alloc([128, D], F32) for _ in range(4)]
        pin = [A.alloc([128, 256], F32) for _ in range(4)]
        hb = [A.alloc([128, D], BF16) for _ in range(2)]
        pb = [A.alloc([128, 256], BF16) for _ in range(2)]
        hT = [A.alloc([128, 8, 128], BF16) for _ in range(2)]
        pT = [A.alloc([128, 2, 128], BF16) for _ in range(2)]
        sgt = [A.alloc([128, 512], F32) for _ in range(2)]
        tiles = [(s, t) for s in range(NS) for t in range(NT)]

        def load(g):
            s, t = tiles[g]
            kb.dma((hin[g % 4].ap, out[s, t * 128:(t + 1) * 128, :]), slot=hin[g % 4], wr=[hin[g % 4]])
            kb.dma((pin[g % 4].ap, p[L, s, t * 128:(t + 1) * 128, :]), slot=pin[g % 4], wr=[pin[g % 4]])

        load(0)
        load(1)
        for g, (s, t) in enumerate(tiles):
            if g + 2 < len(tiles):
                load(g + 2)
            h, pi_, hb_, pb_, hT_, pT_ = hin[g % 4], pin[g % 4], hb[g % 2], pb[g % 2], hT[g % 2], pT[g % 2]
            kb.cp("act", hb_.ap, h.ap, rd=[h], wr=[hb_])
            kb.cp("pool", pb_.ap, pi_.ap, rd=[pi_], wr=[pb_])
            pst, pst2 = PS[g % 2], PS[2 + g % 2]
            for k in range(8):
                kb.tr(pst.bf[:, k * 128:(k + 1) * 128], hb_.ap[:, k * 128:(k + 1) * 128], ident.ap, rd=[hb_], wr=[pst])
            for k in range(2):
                kb.tr(pst2.bf[:, k * 128:(k + 1) * 128], pb_.ap[:, k * 128:(k + 1) * 128], ident.ap, rd=[pb_], wr=[pst2])
            kb.cp("dve", hT_.ap.rearrange("p a b -> p (a b)"), pst.bf, rd=[pst], wr=[hT_])
            kb.cp("act", pT_.ap.rearrange("p a b -> p (a b)"), pst2.bf[:, 0:256], rd=[pst2], wr=[pT_])
            for n in range(2):
                psg, psp = PS[4 + n], PS[6 + n]
                sg = sgt[(2 * g + n) % 2]
                for k in range(8):
                    kb.mm(psg.ap, hT_.ap[:, k, :], Wpg.ap[:, k, n * 512:(n + 1) * 512], k == 0, k == 7, rd=[hT_], wr=[psg])
                for k in range(2):
                    kb.mm(psp.ap, pT_.ap[:, k, :], Wpp.ap[:, k, n * 512:(n + 1) * 512], k == 0, k == 1, rd=[pT_], wr=[psp])
                kb.act(sg.ap, psg.ap, AF.Sigmoid, rd=[psg], wr=[sg])
                kb.tt("dve", sg.ap, sg.ap, psp.ap, ALU.mult, rd=[psp], wr=[sg])
                kb.tt("dve", h.ap[:, n * 512:(n + 1) * 512], h.ap[:, n * 512:(n + 1) * 512], sg.ap, ALU.add, rd=[sg], wr=[h])
            kb.dma((out[s, t * 128:(t + 1) * 128, :], h.ap), slot=h, rd=[h])
        kb.barrier()

    def phase_gla_stub(L):
        A.off = MARK
        z = A.alloc([128, 512], BF16)
        kb.memset("dve", z.ap, 0.0, wr=[z])
        for s in range(NS):
            for t in range(NT):
                kb.dma((MIX[s, t * 128:(t + 1) * 128, 1024:1536], z.ap), slot=z, rd=[z])
        kb.barrier()

    if os.environ.get("GLA_REAL", "1") != "1":
        phase_gla = phase_gla_stub
    return dict(nc=nc, kb=kb, a1=phase_a1, da=phase_da, ssd=phase_ssd, gla=phase_gla, a3=phase_a3, ffn=phase_ffn, ple=phase_ple,
                x=x, out=out)


def build_full(S, NS, NL, dbg=False, phases=None):
    b = build(S, NS, NL, dbg=dbg)
    for L in range(NL):
        hsrc = b["x"] if L == 0 else b["out"]
        for name in ("a1", "da", "ssd", "gla", "a3", "ffn", "ple"):
            if phases is not None and name not in phases:
                continue
            if name in ("a1", "a3"):
                b[name](L, hsrc)
            else:
                b[name](L)
    b["kb"].barrier()
    b["kb"].emit()
    return b["nc"], b["kb"]


def make_consts():
    tri = np.triu(np.ones((128, 128), np.float32))
    invf = (500000.0 ** (-np.arange(0, 16, 2, dtype=np.float32) / 16.0)).astype(np.float32)
    return {
        "c_ident": np.eye(128, dtype=np.float32).astype(ml_dtypes.bfloat16),
        "c_tri": tri.astype(ml_dtypes.bfloat16),
        "c_trif": tri,
        "c_onesf": np.ones((128, 128), np.float32),
        "c_onesb": np.ones((128, 128), np.float32).astype(ml_dtypes.bfloat16),
        "c_invf": np.ascontiguousarray(np.broadcast_to(invf[None, :], (128, 8))),
    }


WNAMES = ["attn_norm", "w_in", "da_q_norm", "da_k_norm", "da_lambda_q1", "da_lambda_k1", "da_lambda_q2", "da_lambda_k2",
          "da_sub_norm", "ssd_conv_w", "ssd_conv_b", "ssd_dt_bias", "ssd_a_log", "ssd_d", "ssd_norm", "gla_w_gate2",
          "gla_b_gate", "gla_norm", "w_out", "ffn_norm", "w_ffn_gate", "w_ffn_up", "w_ffn_down", "ple_w_proj", "ple_w_gate"]

_CACHE = {}


def run(inputs, S, NS, NL, ncores, dbg=False, phases=None):
    key = (S, NS, NL, dbg, tuple(phases) if phases else None)
    if key not in _CACHE:
        _CACHE[key] = build_full(S, NS, NL, dbg=dbg, phases=phases)
    nc, kb = _CACHE[key]
    consts = make_consts()
    in_maps = []
    for c in range(ncores):
        m = dict(consts)
        m["x"] = np.ascontiguousarray(np.asarray(inputs["x"], np.float32)[c * NS:(c + 1) * NS])
        m["p"] = np.ascontiguousarray(np.asarray(inputs["p"], np.float32)[:NL, c * NS:(c + 1) * NS])
        m["positions"] = np.ascontiguousarray(np.asarray(inputs["positions"], np.int32)[c * NS:(c + 1) * NS])
        for w in WNAMES:
            m[w] = np.ascontiguousarray(np.asarray(inputs[w], np.float32)[:NL])
        in_maps.append(m)
    res = run_bass_kernel_spmd(nc, in_maps, core_ids=list(range(ncores)))
    return res


def kernel(**inputs):
    res = run(inputs, 4096, 2, 4, 8)
    return np.concatenate([np.asarray(r["out"], np.float32) for r in res.results], axis=0)
```

```python
import math
import numpy as np
import ml_dtypes
import concourse.bass as bass
import concourse.mybir as mybir
from concourse.bass_utils import run_bass_kernel_spmd

F32, BF16, I32 = mybir.dt.float32, mybir.dt.bfloat16, mybir.dt.int32
AF = mybir.ActivationFunctionType
ALU = mybir.AluOpType
AX = mybir.AxisListType

D = 1024
IN_COLS = 4632
DFF = 2816
EPS = 1e-6
NDS = 56
import os
DA_STOP = int(os.environ.get('DA_STOP', '0'))
DA_SKIP = os.environ.get('DA_SKIP', '')
DA_CUT = int(os.environ.get('DA_CUT', '9'))
GLA_CUT = int(os.environ.get('GLA_CUT', '9'))
POOL_TT = os.environ.get('POOL_TT', '1') == '1'
SEQ_SKEW = int(os.environ.get('SEQ_SKEW', '2'))
FUSE_GLA = os.environ.get('FUSE_GLA', '1') == '1'
GLA_SUB = int(os.environ.get('GLA_SUB', '9'))


class Res:
    __slots__ = ("w", "r")

    def __init__(self):
        self.w = None
        self.r = {}


class Tile:
    def __init__(self, ap):
        self.ap = ap
        self.res = Res()
        self.dsem = None

    def __getitem__(self, k):
        return self.ap[k]


class KB:
    COMP = ("pe", "act", "dve", "pool")
    ENGS = ("pe", "act", "dve", "pool", "sp")

    def __init__(self, nc):
        self.nc = nc
        self.ops = {e: [] for e in self.ENGS}
        self.sem = {e: nc.alloc_semaphore("s_" + e) for e in self.COMP}
        self.cnt = {e: 0 for e in self.COMP}
        self.seen = {e: {} for e in self.ENGS}
        self.dsems = [[nc.alloc_semaphore("d%d" % i), 0] for i in range(NDS)]
        self.dnext = 0
        self.tiles = []
        self.nops = 0

    def track(self, t):
        self.tiles.append(t)
        return t

    def _filter(self, eng, deps):
        best = {}
        for (s, v) in deps:
            if eng == "pe" and s is self.sem["pe"]:
                continue
            if self.seen[eng].get(s.num, 0) >= v:
                continue
            if best.get(s.num, (None, 0))[1] < v:
                best[s.num] = (s, v)
        for k, (s, v) in best.items():
            self.seen[eng][k] = v
        return list(best.values())

    def _deps(self, rd, wr):
        deps = []
        for t in rd:
            if t.res.w:
                deps.append(t.res.w)
        for t in wr:
            if t.res.w:
                deps.append(t.res.w)
            deps.extend(t.res.r.values())
        return deps

    def _commit(self, ev, rd, wr):
        wrs = set(id(t) for t in wr)
        for t in rd:
            if id(t) in wrs:
                continue
            cur = t.res.r.get(ev[0].num)
            if cur is None or cur[1] < ev[1]:
                t.res.r[ev[0].num] = ev
        for t in wr:
            t.res.w = ev
            t.res.r = {}

    def op(self, eng, fn, rd=(), wr=(), sig=True):
        waits = self._filter(eng, self._deps(rd, wr))
        if sig:
            self.cnt[eng] += 1
            ev = (self.sem[eng], self.cnt[eng])
            inc = (self.sem[eng], 1)
        else:
            ev = (self.sem[eng], self.cnt[eng] + 1)
            inc = None
        self.ops[eng].append((waits, fn, inc))
        self._commit(ev, rd, wr)
        self.nops += 1

    def dma(self, pairs, slot, rd=(), wr=(), q="sp", slow=False):
        if not isinstance(pairs, list):
            pairs = [pairs]
        if slot.dsem is None:
            slot.dsem = self.dsems[self.dnext % NDS]
            self.dnext += 1
        d = slot.dsem
        deps = self._deps(rd, wr)
        if d[1] > 0:
            deps.append((d[0], d[1]))
        waits = self._filter(q, deps)
        for (o, i) in pairs:
            if slow:
                fn = (lambda e, o=o, i=i: e.dma_start(out=o, in_=i, allow_slow_non_contiguous=True))
            else:
                fn = (lambda e, o=o, i=i: e.dma_start(out=o, in_=i))
            self.ops[q].append((waits, fn, (d[0], 16)))
            waits = []
            d[1] += 16
        self._commit((d[0], d[1]), rd, wr)
        self.nops += len(pairs)

    def barrier(self):
        evs = [(self.sem[e], self.cnt[e]) for e in self.COMP if self.cnt[e] > 0]
        evs += [(d[0], d[1]) for d in self.dsems if d[1] > 0]
        for e in self.ENGS:
            waits = self._filter(e, evs)
            if waits:
                self.ops[e].append((waits, None, None))
        for t in self.tiles:
            t.res.w = None
            t.res.r = {}
            t.dsem = None
        self.tiles = []

    def emit(self):
        def mk(name):
            def body(e):
                for waits, fn, inc in self.ops[name]:
                    for (s, v) in waits:
                        e.wait_ge(s, v)
                    if fn is not None:
                        ins = fn(e)
                        if inc is not None:
                            ins.then_inc(inc[0], inc[1])
            return body

        with self.nc.Block() as block:
            block.tensor(mk("pe"))
            block.scalar(mk("act"))
            block.vector(mk("dve"))
            block.gpsimd(mk("pool"))
            block.sync(mk("sp"))

    def mm(self, out, lhsT, rhs, start, stop, rd, wr, sig=None):
        if sig is None:
            sig = stop
        self.op("pe", lambda e: e.matmul(out, lhsT, rhs, start=start, stop=stop), rd, wr, sig)

    def tr(self, out, in_, ident, rd, wr):
        self.op("pe", lambda e: e.transpose(out, in_, ident), rd, wr)

    def act(self, out, in_, func, rd, wr, **kw):
        self.op("act", lambda e: e.activation(out=out, in_=in_, func=func, **kw), rd, wr)

    def tt(self, eng, out, a, b, op, rd, wr):
        if eng == "pool" and not POOL_TT:
            eng = "dve"
        self.op(eng, lambda e: e.tensor_tensor(out, a, b, op), rd, wr)

    def ts(self, eng, out, a, s1, s2, op0, op1, rd, wr):
        if s2 is None:
            self.op(eng, lambda e: e.tensor_scalar(out, a, s1, None, op0), rd, wr)
        else:
            self.op(eng, lambda e: e.tensor_scalar(out, a, s1, s2, op0, op1), rd, wr)

    def stt(self, eng, out, in0, scalar, in1, op0, op1, rd, wr):
        self.op(eng, lambda e: e.scalar_tensor_tensor(out, in0, scalar, in1, op0, op1), rd, wr)

    def cp(self, eng, out, in_, rd, wr):
        if eng == "act":
            self.op("act", lambda e: e.activation(out=out, in_=in_, func=AF.Copy), rd, wr)
        else:
            self.op(eng, lambda e: e.tensor_copy(out, in_), rd, wr)

    def red(self, eng, out, in_, rd, wr):
        self.op(eng, lambda e: e.tensor_reduce(out, in_, AX.X, ALU.add), rd, wr)

    def memset(self, eng, ap, val, wr):
        self.op(eng, lambda e: e.memset(ap, val), (), wr)


class Arena:
    def __init__(self, nc, kb, nbytes):
        self.t = nc.alloc_sbuf_tensor("arena", [128, nbytes // 2], BF16)
        self.off = 0
        self.cap = nbytes
        self.kb = kb

    def alloc(self, shape, dt, persist=False):
        n = 1
        for s in shape[1:]:
            n *= s
        esz = 4 if dt in (F32, I32) else 2
        b = (n * esz + 31) // 32 * 32
        if self.off + b > self.cap:
            raise RuntimeError("arena overflow: need %d at %d cap %d" % (b, self.off, self.cap))
        ap = self.t[:, self.off // 2:(self.off + b) // 2]
        self.off += b
        if dt != BF16:
            ap = ap.bitcast(dt)
        ap = ap[0:shape[0], 0:n]
        if len(shape) == 3:
            ap = ap.rearrange("p (a b) -> p a b", a=shape[1])
        elif len(shape) == 4:
            ap = ap.rearrange("p (a b c) -> p a b c", a=shape[1], b=shape[2])
        t = Tile(ap)
        if not persist:
            self.kb.track(t)
        return t


class Cfg:
    pass


def build(S, NS, NL, dbg=False):
    nc = bass.Bass("TRN2", target_bir_lowering=False)
    NT = S // 128
    c = Cfg()

    def din(name, shape, dt=F32):
        return nc.dram_tensor(name, list(shape), dt, kind="ExternalInput").ap()

    x = din("x", [NS, S, D])
    p = din("p", [NL, NS, S, 256])
    pos = din("positions", [NS, S], I32)
    attn_norm = din("attn_norm", [NL, D])
    w_in = din("w_in", [NL, D, IN_COLS])
    da_q_norm = din("da_q_norm", [NL, 64])
    da_k_norm = din("da_k_norm", [NL, 64])
    lq1 = din("da_lambda_q1", [NL, 64])
    lk1 = din("da_lambda_k1", [NL, 64])
    lq2 = din("da_lambda_q2", [NL, 64])
    lk2 = din("da_lambda_k2", [NL, 64])
    da_sub_norm = din("da_sub_norm", [NL, 128])
    ssd_conv_w = din("ssd_conv_w", [NL, 4, 1024])
    ssd_conv_b = din("ssd_conv_b", [NL, 1024])
    ssd_dt_bias = din("ssd_dt_bias", [NL, 8])
    ssd_a_log = din("ssd_a_log", [NL, 8])
    ssd_d = din("ssd_d", [NL, 8])
    ssd_norm = din("ssd_norm", [NL, 512])
    gla_w_gate2 = din("gla_w_gate2", [NL, 16, 256])
    gla_b_gate = din("gla_b_gate", [NL, 256])
    gla_norm = din("gla_norm", [NL, 128])
    w_out = din("w_out", [NL, 1536, D])
    ffn_norm = din("ffn_norm", [NL, D])
    w_ffn_gate = din("w_ffn_gate", [NL, D, DFF])
    w_ffn_up = din("w_ffn_up", [NL, D, DFF])
    w_ffn_down = din("w_ffn_down", [NL, DFF, D])
    ple_w_proj = din("ple_w_proj", [NL, 256, D])
    ple_w_gate = din("ple_w_gate", [NL, D, D])
    c_ident = din("c_ident", [128, 128], BF16)
    c_tri = din("c_tri", [128, 128], BF16)
    c_trif = din("c_trif", [128, 128])
    c_onesf = din("c_onesf", [128, 128])
    c_invf = din("c_invf", [128, 8])
    c_onesb = din("c_onesb", [128, 128], BF16)

    out = nc.dram_tensor("out", [NS, S, D], F32, kind="ExternalOutput").ap()
    okind = "ExternalOutput" if dbg else "Internal"
    PT = nc.dram_tensor("PT", [NS, S, 3584], BF16, kind=okind).ap()
    XT = nc.dram_tensor("XT", [NS, 1024, S], BF16, kind=okind).ap()
    LRT = nc.dram_tensor("LRT", [NS, 16, S], BF16, kind=okind).ap()
    DTS = nc.dram_tensor("DTS", [NS, S, 8], F32, kind=okind).ap()
    MIX = nc.dram_tensor("MIX", [NS, S, 1536], BF16, kind=okind).ap()

    kb = KB(nc)
    A = Arena(nc, kb, 207 * 1024)
    PS = []
    for i in range(8):
        t = Tile(nc.alloc_psum_tensor("ps%d" % i, [128, 512], F32)[:, :])
        t.bf = t.ap.bitcast(BF16)
        PS.append(t)

    ident = A.alloc([128, 128], BF16, persist=True)
    tri = A.alloc([128, 128], BF16, persist=True)
    trif = A.alloc([128, 128], F32, persist=True)
    onesf = A.alloc([128, 128], F32, persist=True)
    invf = A.alloc([128, 8], F32, persist=True)
    junk = A.alloc([128, 1024], BF16, persist=True)
    onesb = A.alloc([128, 128], BF16, persist=True)
    for t, src in ((ident, c_ident), (tri, c_tri), (trif, c_trif), (onesf, c_onesf), (invf, c_invf), (onesb, c_onesb)):
        kb.dma((t.ap, src), slot=t, wr=[t])
    kb.barrier()
    MARK = A.off
    cvt_i = [0]

    def load_w(dst, src, nk, ncols, stage, rows=128):
        CH = stage[0].ap.shape[1]
        for k in range(nk):
            for c0 in range(0, ncols, CH):
                n = min(CH, ncols - c0)
                st = stage[cvt_i[0] % len(stage)]
                eng = ("pool", "dve", "act")[cvt_i[0] % 3]
                cvt_i[0] += 1
                kb.dma((st.ap[0:rows, 0:n], src[k * rows:(k + 1) * rows, c0:c0 + n]), slot=st, wr=[st])
                kb.cp(eng, dst.ap[0:rows, k, c0:c0 + n], st.ap[0:rows, 0:n], rd=[st], wr=[])

    def bcast_load(t, src, eng="pool", mul=None):
        kb.dma((t.ap, src.partition_broadcast(128)), slot=t, wr=[t])
        if mul is not None:
            kb.ts(eng, t.ap, t.ap, float(mul), None, ALU.mult, None, rd=[], wr=[t])

    def rstd_from_ss(out_ap, ss_ap, n, rd, wr):
        kb.act(out_ap, ss_ap, AF.Ln, rd, wr, bias=float(n * EPS))
        kb.act(out_ap, out_ap, AF.Exp, (), wr, scale=-0.5)

    def split_hl(hi_ap, lo_ap, src_ap, rd, wr):
        kb.cp("dve", hi_ap, src_ap, rd, wr)
        kb.tt("dve", lo_ap, src_ap, hi_ap, ALU.subtract, rd, wr)

    ev_i = [0]

    def evac(out_ap, in_ap, rd, wr):
        eng = ("act", "dve")[ev_i[0] % 2]
        ev_i[0] += 1
        kb.cp(eng, out_ap, in_ap, rd, wr)


    def seq_driver(factories, skew=3, depth=2):
        active = []
        it = iter(factories)
        pending = next(it, None)
        while active or pending is not None:
            if pending is not None and (not active or (len(active) < depth and active[-1][1] >= skew)):
                active.append([pending(), 0])
                pending = next(it, None)
            for a in list(active):
                try:
                    next(a[0])
                    a[1] += 1
                except StopIteration:
                    active.remove(a)
            yield

    def phase_a1(L, hsrc):
        A.off = MARK
        Win = A.alloc([128, 8, IN_COLS], BF16)
        stage = [A.alloc([128, 1024], F32) for _ in range(3)]
        gain = A.alloc([128, D], F32)
        load_w(Win, w_in[L], 8, IN_COLS, stage)
        bcast_load(gain, attn_norm[L], mul=32.0)
        kb.barrier()
        hin = [A.alloc([128, D], F32) for _ in range(6)]
        abf = [A.alloc([128, D], BF16) for _ in range(2)]
        aT = [A.alloc([128, 8, 512], BF16) for _ in range(2)]
        ptm = [A.alloc([128, 3584], BF16) for _ in range(3)]
        dtsb = [A.alloc([128, 8], F32) for _ in range(3)]
        xts = [A.alloc([128, 8, 512], BF16) for _ in range(2)]
        lrs = [A.alloc([16, 512], BF16) for _ in range(2)]
        ss = [A.alloc([128, 2], F32) for _ in range(4)]
        tiles = [(s, t) for s in range(NS) for t in range(NT)]
        TM = [(0, 0, 512), (512, 512, 512), (1024, 1024, 512), (1536, 1536, 512),
              (3080, 2048, 512), (3592, 2560, 512), (4104, 3072, 512)]

        def load(g):
            s, t = tiles[g]
            h = hin[g % 6]
            kb.dma((h.ap, hsrc[s, t * 128:(t + 1) * 128, :]), slot=h, wr=[h])

        for g in range(min(5, len(tiles))):
            load(g)
        pi = 0
        for g, (s, t) in enumerate(tiles):
            if g + 5 < len(tiles):
                load(g + 5)
            j = t % 4
            u = g // 4
            h = hin[g % 6]
            sq = ss[g % 4]
            ab = abf[g % 2]
            at = aT[u % 2]
            kb.act(junk.ap, h.ap, AF.Square, rd=[h], wr=[junk, sq], accum_out=sq.ap[:, 0:1])
            rstd_from_ss(sq.ap[:, 1:2], sq.ap[:, 0:1], D, rd=[], wr=[sq])
            kb.stt("dve", ab.ap, h.ap, sq.ap[:, 1:2], gain.ap, ALU.mult, ALU.mult, rd=[h, sq, gain], wr=[ab])
            pst = PS[pi % 2]
            pi += 1
            for k in range(8):
                kb.tr(pst.bf[:, k * 128:(k + 1) * 128], ab.ap[:, k * 128:(k + 1) * 128], ident.ap, rd=[ab], wr=[pst])
            evac(at.ap[:, :, j * 128:(j + 1) * 128], pst.bf.rearrange("p (a b) -> p a b", a=8), rd=[pst], wr=[at])
            pt = ptm[g % 3]
            dt_ = dtsb[g % 3]
            for ci, (c0, d0, n) in enumerate(TM):
                ps = PS[2 + (ci % 4)]
                for k in range(8):
                    kb.mm(ps.ap[:, 0:n], at.ap[:, k, j * 128:(j + 1) * 128], Win.ap[:, k, c0:c0 + n],
                          k == 0, k == 7, rd=[at], wr=[ps])
                evac(pt.ap[:, d0:d0 + n], ps.ap[:, 0:n], rd=[ps], wr=[pt])
            ps = PS[6]
            for k in range(8):
                kb.mm(ps.ap[:, 0:8], at.ap[:, k, j * 128:(j + 1) * 128], Win.ap[:, k, 3072:3080], k == 0, k == 7,
                      rd=[at], wr=[ps])
            kb.cp("dve", dt_.ap, ps.ap[:, 0:8], rd=[ps], wr=[dt_])
            kb.dma((PT[s, t * 128:(t + 1) * 128, :], pt.ap), slot=pt, rd=[pt])
            kb.dma((DTS[s, t * 128:(t + 1) * 128, :], dt_.ap), slot=dt_, rd=[dt_])
            if j == 3:
                t0 = (t - 3) * 128
                xs = xts[u % 2]
                lr = lrs[u % 2]
                for b in range(8):
                    ps = PS[2 + (b % 4)]
                    for k in range(8):
                        kb.mm(ps.ap, Win.ap[:, k, 2048 + b * 128:2048 + (b + 1) * 128], at.ap[:, k, :], k == 0, k == 7,
                              rd=[at], wr=[ps])
                    evac(xs.ap[:, b, :], ps.ap, rd=[ps], wr=[xs])
                ps = PS[7]
                for k in range(8):
                    kb.mm(ps.ap[0:16, :], Win.ap[:, k, 4616:4632], at.ap[:, k, :], k == 0, k == 7, rd=[at], wr=[ps])
                evac(lr.ap, ps.ap[0:16, :], rd=[ps], wr=[lr])
                kb.dma((XT[s].rearrange("(b p) t -> p b t", p=128)[:, :, t0:t0 + 512], xs.ap), slot=xs, rd=[xs])
                kb.dma((LRT[s, :, t0:t0 + 512], lr.ap), slot=lr, rd=[lr])
        kb.barrier()

    def phase_da(L):
        lam_init = 0.8 - 0.6 * math.exp(-0.3 * L)
        for s in range(NS):
            A.off = MARK
            qT = A.alloc([128, 4, S], BF16)
            kT = A.alloc([128, 4, S], BF16)
            Vp = A.alloc([128, NT, 4, 130], BF16)
            gqk = A.alloc([128, 16, 64], F32)
            gsub = A.alloc([128, 128], F32)
            lamt = A.alloc([128, 4, 64], F32)
            lamv = A.alloc([128, 8], F32)
            posi = A.alloc([128, NT], I32)
            posf = A.alloc([128, NT], F32)
            cs = A.alloc([128, NT, 16], F32)
            for r in range(8):
                kb.dma((gqk.ap[:, r, :], da_q_norm[L].partition_broadcast(128)), slot=gqk, wr=[gqk])
                kb.dma((gqk.ap[:, 8 + r, :], da_k_norm[L].partition_broadcast(128)), slot=gqk, wr=[gqk])
            kb.ts("pool", gqk.ap, gqk.ap, 8.0, None, ALU.mult, None, rd=[], wr=[gqk])
            bcast_load(gsub, da_sub_norm[L], mul=math.sqrt(128.0) * (1.0 - lam_init))
            for i_, src in enumerate((lq1, lk1, lq2, lk2)):
                kb.dma((lamt.ap[:, i_, :], src[L].partition_broadcast(128)), slot=lamt, wr=[lamt])
            kb.tt("dve", lamt.ap[:, 0, :], lamt.ap[:, 0, :], lamt.ap[:, 1, :], ALU.mult, rd=[], wr=[lamt])
            kb.tt("dve", lamt.ap[:, 2, :], lamt.ap[:, 2, :], lamt.ap[:, 3, :], ALU.mult, rd=[], wr=[lamt])
            kb.red("dve", lamv.ap[:, 0:1], lamt.ap[:, 0, :], rd=[lamt], wr=[lamv])
            kb.red("dve", lamv.ap[:, 1:2], lamt.ap[:, 2, :], rd=[lamt], wr=[lamv])
            kb.act(lamv.ap[:, 2:4], lamv.ap[:, 0:2], AF.Exp, rd=[], wr=[lamv])
            kb.tt("dve", lamv.ap[:, 4:5], lamv.ap[:, 3:4], lamv.ap[:, 2:3], ALU.subtract, rd=[], wr=[lamv])
            kb.ts("dve", lamv.ap[:, 5:6], lamv.ap[:, 4:5], float(-lam_init), None, ALU.add, None, rd=[], wr=[lamv])
            kb.dma([(posi.ap[:, t_:t_ + 1], pos[s, t_ * 128:(t_ + 1) * 128].rearrange("(p o) -> p o", o=1)) for t_ in range(NT)],
                   slot=posi, wr=[posi])
            kb.cp("dve", posf.ap, posi.ap, rd=[posi], wr=[posf])
            kb.tt("dve", cs.ap[:, :, 8:16], posf.ap.unsqueeze(2).to_broadcast([128, NT, 8]),
                  invf.ap.unsqueeze(1).to_broadcast([128, NT, 8]), ALU.mult, rd=[posf], wr=[cs])
            kb.ts("dve", cs.ap[:, :, 0:8], cs.ap[:, :, 8:16], 0.5 * math.pi, None, ALU.add, None, rd=[], wr=[cs])
            ki_ = A.alloc([128, NT, 16], I32)
            kf_ = A.alloc([128, NT, 16], F32)
            kb.ts("dve", ki_.ap, cs.ap, 1.0 / (2 * math.pi), None, ALU.mult, None, rd=[cs], wr=[ki_])
            kb.cp("dve", kf_.ap, ki_.ap, rd=[ki_], wr=[kf_])
            kb.stt("dve", cs.ap, kf_.ap, -2 * math.pi, cs.ap, ALU.mult, ALU.add, rd=[kf_], wr=[cs])
            kb.ts("dve", kf_.ap, cs.ap, math.pi, None, ALU.is_gt, None, rd=[cs], wr=[kf_])
            kb.stt("dve", cs.ap, kf_.ap, -2 * math.pi, cs.ap, ALU.mult, ALU.add, rd=[kf_], wr=[cs])
            kb.ts("dve", kf_.ap, cs.ap, -math.pi, None, ALU.is_lt, None, rd=[cs], wr=[kf_])
            kb.stt("dve", cs.ap, kf_.ap, 2 * math.pi, cs.ap, ALU.mult, ALU.add, rd=[kf_], wr=[cs])
            kb.ts("dve", cs.ap, cs.ap, math.pi, -math.pi, ALU.min, ALU.max, rd=[], wr=[cs])
            kb.act(cs.ap, cs.ap, AF.Sin, rd=[], wr=[cs])
            kb.memset("pool", Vp.ap[:, :, :, 128:130], 1.0, wr=[Vp])
            if DA_STOP == 1:
                kb.barrier()
                continue
            mark_pro = A.off
            qk = [A.alloc([128, 16, 64], BF16) for _ in range(4)]
            sqb = [A.alloc([128, 16, 64], F32) for _ in range(2)]
            qn = [A.alloc([128, 16, 64], F32) for _ in range(2)]
            qb = [A.alloc([128, 16, 64], BF16) for _ in range(2)]
            st16 = [A.alloc([128, 32], F32) for _ in range(2)]
            rt = [A.alloc([128, 4, 16, 8], F32) for _ in range(2)]

            for t in range(NT):
                kb.dma((Vp.ap[:, t, :, 0:128], PT[s, t * 128:(t + 1) * 128, 1024:1536].rearrange("p (h d) -> p h d", h=4)),
                       slot=Vp, wr=[Vp])

            def pro_gen(t):
                q_, sq_, qn_, qb_, st_, rt_ = qk[t % 4], sqb[t % 2], qn[t % 2], qb[t % 2], st16[t % 2], rt[t % 2]
                kb.tt("dve", sq_.ap, q_.ap, q_.ap, ALU.mult, rd=[q_], wr=[sq_])
                yield
                kb.red("dve", st_.ap[:, 0:16], sq_.ap, rd=[sq_], wr=[st_])
                yield
                rstd_from_ss(st_.ap[:, 16:32], st_.ap[:, 0:16], 64, rd=[], wr=[st_])
                yield
                kb.tt("dve", qn_.ap, q_.ap, st_.ap[:, 16:32].unsqueeze(2).to_broadcast([128, 16, 64]), ALU.mult,
                      rd=[q_, st_], wr=[qn_])
                yield
                kb.tt("dve", qn_.ap, qn_.ap, gqk.ap, ALU.mult, rd=[gqk], wr=[qn_])
                yield
                kb.cp("act", qb_.ap, qn_.ap, rd=[qn_], wr=[qb_])
                cb_ = cs.ap[:, t, 0:8].unsqueeze(1).to_broadcast([128, 16, 8])
                sb_ = cs.ap[:, t, 8:16].unsqueeze(1).to_broadcast([128, 16, 8])
                x1 = qn_.ap[:, :, 0:8]
                x2 = qn_.ap[:, :, 8:16]
                kb.tt("dve", rt_.ap[:, 0], x1, cb_, ALU.mult, rd=[qn_, cs], wr=[rt_])
                kb.tt("dve", rt_.ap[:, 1], x2, sb_, ALU.mult, rd=[qn_, cs], wr=[rt_])
                kb.tt("dve", rt_.ap[:, 2], x2, cb_, ALU.mult, rd=[qn_, cs], wr=[rt_])
                kb.tt("dve", rt_.ap[:, 3], x1, sb_, ALU.mult, rd=[qn_, cs], wr=[rt_])
                yield
                kb.tt("dve", qb_.ap[:, :, 0:8], rt_.ap[:, 0], rt_.ap[:, 1], ALU.subtract, rd=[rt_], wr=[qb_])
                kb.tt("dve", qb_.ap[:, :, 8:16], rt_.ap[:, 2], rt_.ap[:, 3], ALU.add, rd=[rt_], wr=[qb_])
                yield
                pst = PS[t % 2]
                pst2 = PS[2 + t % 2]
                qbf = qb_.ap.rearrange("p a b -> p (a b)")
                for k in range(4):
                    kb.tr(pst.bf[:, k * 128:(k + 1) * 128], qbf[:, k * 128:(k + 1) * 128], ident.ap, rd=[qb_], wr=[pst])
                for k in range(4):
                    kb.tr(pst2.bf[:, k * 128:(k + 1) * 128], qbf[:, (4 + k) * 128:(5 + k) * 128], ident.ap, rd=[qb_], wr=[pst2])
                kb.cp("act", qT.ap[:, :, t * 128:(t + 1) * 128], pst.bf[:, 0:512].rearrange("p (a b) -> p a b", a=4), rd=[pst], wr=[qT])
                kb.cp("dve", kT.ap[:, :, t * 128:(t + 1) * 128], pst2.bf[:, 0:512].rearrange("p (a b) -> p a b", a=4), rd=[pst2], wr=[kT])

            def loadp(t):
                if t < NT:
                    q_ = qk[t % 4]
                    kb.dma((q_.ap.rearrange("p a b -> p (a b)"), PT[s, t * 128:(t + 1) * 128, 0:1024]), slot=q_, wr=[q_])

            loadp(0)
            loadp(1)
            for t in range(0, NT, 2):
                loadp(t + 2)
                loadp(t + 3)
                gens = [pro_gen(t), pro_gen(t + 1)]
                while gens:
                    for g_ in list(gens):
                        try:
                            next(g_)
                        except StopIteration:
                            gens.remove(g_)
            if DA_STOP == 2:
                kb.barrier()
                continue
            kb.barrier()
            A.off = mark_pro
            E = [A.alloc([128, 512], BF16) for _ in range(3)]
            osb = [A.alloc([128, 2, 4, 130], F32) for _ in range(2)]
            rr = [A.alloc([128, 16], F32) for _ in range(2)]
            o_ = [A.alloc([128, 4, 128], F32) for _ in range(2)]
            t2 = [A.alloc([128, 4, 128], F32) for _ in range(2)]
            yda = [A.alloc([128, 4, 4, 128], BF16) for _ in range(2)]
            it = 0
            blks = [(Q, h, c_, kt) for Q in range(S // 512) for h in range(4) for c_ in range(2) for kt in range(4 * Q + 4)]
            qk_next = [0]

            def emit_qk():
                n = qk_next[0]
                if n >= len(blks):
                    return
                qk_next[0] += 1
                Q, h, c_, kt = blks[n]
                i = kt - 4 * Q
                q0 = max(i, 0) * 128
                p0, p1 = c_ * 64, (c_ + 1) * 64
                psS = PS[n % 2]
                e_ = E[n % 3]
                kb.mm(psS.ap[:, q0:512], kT.ap[p0:p1, h, kt * 128:(kt + 1) * 128],
                      qT.ap[p0:p1, h, Q * 512 + q0:(Q + 1) * 512], True, True, rd=[kT, qT], wr=[psS])
                kb.act(e_.ap[:, q0:512], psS.ap[:, q0:512], AF.Exp, rd=[psS], wr=[e_], scale=0.125)
                if i >= 0:
                    kb.tt("dve", e_.ap[:, q0:q0 + 128], e_.ap[:, q0:q0 + 128], tri.ap, ALU.mult, rd=[], wr=[e_])

            gla_g = [phase_gla(L, fused_s=s) if FUSE_GLA else None]
            gla_every = max(1, (len(blks) * 9) // (NT * 35))

            def step_gla():
                if gla_g[0] is not None:
                    try:
                        next(gla_g[0])
                    except StopIteration:
                        gla_g[0] = None

            step_gla()
            emit_qk()
            n_blk = 0
            for Q in range(S // 512):
                yd = yda[Q % 2]
                for h in range(4):
                    ob = osb[it % 2]
                    for c_ in range(2):
                        accA = PS[2 + 2 * ((2 * it + c_) % 2)]
                        accB = PS[3 + 2 * ((2 * it + c_) % 2)]
                        accs = [accA.ap[:, 0:129], accA.ap[:, 130:259], accB.ap[:, 0:129], accB.ap[:, 130:259]]
                        acct = [accA, accA, accB, accB]
                        nkt = 4 * Q + 4
                        for kt in range(nkt):
                            assert blks[n_blk] == (Q, h, c_, kt)
                            e_ = E[n_blk % 3]
                            n_blk += 1
                            emit_qk()
                            if n_blk % gla_every == 0:
                                step_gla()
                            i = kt - 4 * Q
                            jmin = max(i, 0)
                            for j in range(jmin, 4):
                                last = (kt == 4 * Q + j)
                                kb.mm(accs[j], e_.ap[:, j * 128:(j + 1) * 128], Vp.ap[:, kt, h, 0:129], (kt == 0 and j % 2 == 0), last,
                                      rd=[e_, Vp], wr=[acct[j]], sig=(last or j == 3))
                        kb.cp("act", ob.ap[:, c_, 0:2, :].rearrange("p a b -> p (a b)"), accA.ap[:, 0:260], rd=[accA], wr=[ob])
                        kb.cp("dve", ob.ap[:, c_, 2:4, :].rearrange("p a b -> p (a b)"), accB.ap[:, 0:260], rd=[accB], wr=[ob])
                    r_ = rr[it % 2]
                    oo = o_[it % 2]
                    tt_ = t2[it % 2]
                    it += 1
                    kb.op("dve", lambda e, a=r_.ap[:, 0:8], b=ob.ap[:, :, :, 128]: e.reciprocal(a.rearrange("p (a b) -> p a b", a=2), b),
                          rd=[ob], wr=[r_])
                    kb.ts("dve", r_.ap[:, 4:8], r_.ap[:, 4:8], lamv.ap[:, 5:6], None, ALU.mult, None, rd=[lamv], wr=[r_])
                    kb.tt("dve", oo.ap, ob.ap[:, 0, :, 0:128], r_.ap[:, 0:4].unsqueeze(2).to_broadcast([128, 4, 128]), ALU.mult,
                          rd=[ob, r_], wr=[oo])
                    kb.tt("dve", tt_.ap, ob.ap[:, 1, :, 0:128], r_.ap[:, 4:8].unsqueeze(2).to_broadcast([128, 4, 128]), ALU.mult,
                          rd=[ob, r_], wr=[tt_])
                    kb.tt("dve", oo.ap, oo.ap, tt_.ap, ALU.add, rd=[tt_], wr=[oo])
                    kb.tt("dve", tt_.ap, oo.ap, oo.ap, ALU.mult, rd=[oo], wr=[tt_])
                    kb.red("dve", r_.ap[:, 8:12], tt_.ap, rd=[tt_], wr=[r_])
                    rstd_from_ss(r_.ap[:, 12:16], r_.ap[:, 8:12], 128, rd=[], wr=[r_])
                    kb.tt("dve", oo.ap, oo.ap, r_.ap[:, 12:16].unsqueeze(2).to_broadcast([128, 4, 128]), ALU.mult, rd=[r_], wr=[oo])
                    kb.tt("dve", yd.ap[:, :, h, :], oo.ap, gsub.ap.unsqueeze(1).to_broadcast([128, 4, 128]), ALU.mult,
                          rd=[oo, gsub], wr=[yd])
                kb.dma((MIX[s, Q * 512:(Q + 1) * 512, 0:512].rearrange("(j p) (h d) -> p j h d", p=128, h=4), yd.ap), slot=yd, rd=[yd])
            while gla_g[0] is not None:
                step_gla()
            kb.barrier()


    def phase_ssd(L):
        A.off = MARK

        def gen(s):
            cw = A.alloc([128, 8, 4], F32)
            cbias = A.alloc([128, 8], F32)
            diag = A.alloc([128, 8, 4, 128], BF16)
            dtb = A.alloc([128, 8], F32)
            abc = A.alloc([128, 8], F32)
            dsm = A.alloc([128, 8], F32)
            dbc = A.alloc([128, 8, 64], F32)
            gn = A.alloc([128, 512], F32)
            H = A.alloc([128, 8, 64], F32)
            prev = A.alloc([128, 512], BF16)
            cwv = ssd_conv_w[L].rearrange("j (b p) -> p b j", p=128)
            for b in range(8):
                kb.dma((cw.ap[:, b, :], cwv[:, b, :]), slot=cw, wr=[cw], slow=True)
            kb.dma([(cbias.ap[:, b_:b_ + 1], ssd_conv_b[L, b_ * 128:(b_ + 1) * 128].rearrange("(p o) -> p o", o=1)) for b_ in range(8)],
                   slot=cbias, wr=[cbias])
            bcast_load(dtb, ssd_dt_bias[L])
            bcast_load(abc, ssd_a_log[L])
            bcast_load(dsm, ssd_d[L])
            bcast_load(gn, ssd_norm[L], mul=16.0)
            kb.act(abc.ap, abc.ap, AF.Exp, rd=[], wr=[abc])
            kb.ts("dve", abc.ap, abc.ap, -1.0, None, ALU.mult, None, rd=[], wr=[abc])
            kb.cp("dve", dbc.ap, dsm.ap.unsqueeze(2).to_broadcast([128, 8, 64]), rd=[dsm], wr=[dbc])
            for b in range(8):
                for j in range(4):
                    kb.ts(("pool", "dve")[j % 2], diag.ap[:, b, j, :], ident.ap, cw.ap[:, b, j:j + 1], None, ALU.mult, None,
                          rd=[cw], wr=[diag])
            kb.memset("dve", H.ap, 0.0, wr=[H])
            kb.memset("pool", prev.ap, 0.0, wr=[prev])
            xTh = [A.alloc([128, 8, 516], BF16) for _ in range(2)]
            xa = [A.alloc([128, 8, 512], BF16) for _ in range(2)]
            zt = [A.alloc([128, 512], BF16) for _ in range(3)]
            dtt = [A.alloc([128, 8], F32) for _ in range(3)]
            xbt = [A.alloc([128, 768], BF16) for _ in range(2)]
            smt = [A.alloc([128, 80], F32) for _ in range(2)]
            dabt = [A.alloc([128, 2, 8, 128], BF16) for _ in range(2)]
            hlt = [A.alloc([128, 16], BF16) for _ in range(2)]
            segt = [A.alloc([128, 8, 128], F32) for _ in range(2)]
            cbmt = [A.alloc([128, 2, 128], F32) for _ in range(2)]
            mtt = [A.alloc([128, 8, 128], BF16) for _ in range(2)]
            xdtt = [A.alloc([128, 8, 64], BF16) for _ in range(2)]
            xdd = [A.alloc([128, 8, 64], BF16) for _ in range(2)]
            xdwt = [A.alloc([128, 8, 64], BF16) for _ in range(2)]
            yt = [A.alloc([128, 512], F32) for _ in range(2)]
            szt = [A.alloc([128, 512], F32) for _ in range(2)]
            nrt = [A.alloc([128, 8], F32) for _ in range(2)]
            yot = [A.alloc([128, 512], BF16) for _ in range(3)]
            XTv = XT[s].rearrange("(b p) t -> p b t", p=128)
            NU = S // 512

            def load_x(U):
                xh = xTh[U % 2]
                if U == 0:
                    kb.memset("pool", xh.ap[:, :, 0:3], 0.0, wr=[xh])
                    kb.dma((xh.ap[:, :, 3:515], XTv[:, :, 0:512]), slot=xh, wr=[xh])
                else:
                    kb.dma((xh.ap[:, :, 0:515], XTv[:, :, U * 512 - 3:U * 512 + 512]), slot=xh, wr=[xh])

            def load_c(t):
                kb.dma((zt[t % 3].ap, PT[s, t * 128:(t + 1) * 128, 1536:2048]), slot=zt[t % 3], wr=[zt[t % 3]])
                kb.dma((dtt[t % 3].ap, DTS[s, t * 128:(t + 1) * 128, :]), slot=dtt[t % 3], wr=[dtt[t % 3]])

            load_x(0)
            load_c(0)
            done = [0]

            def conv_stage(U):
                if U + 1 < NU:
                    load_x(U + 1)
                xh = xTh[U % 2]
                xa_ = xa[U % 2]
                for b in range(8):
                    ps = PS[b % 2]
                    for j in range(4):
                        kb.mm(ps.ap, diag.ap[:, b, j, :], xh.ap[:, b, j:j + 512], j == 0, j == 3, rd=[diag, xh], wr=[ps])
                    kb.act(xa_.ap[:, b, :], ps.ap, AF.Silu, rd=[ps, cbias], wr=[xa_], bias=cbias.ap[:, b:b + 1])

            def chunk_gen(U, cc_only):
                xh = xTh[U % 2]
                xa_ = xa[U % 2]
                if cc_only == 0:
                    conv_stage(U)
                    yield
                for cc in (cc_only,):
                    t = U * 4 + cc
                    cs_ = cc * 128
                    if t + 1 < NT:
                        load_c(t + 1)
                    z, dtr, xb, sm = zt[t % 3], dtt[t % 3], xbt[t % 2], smt[t % 2]
                    dab, seg, cbm, MT = dabt[t % 2], segt[t % 2], cbmt[t % 2], mtt[t % 2]
                    xdt, xd, xdw, y, sz, nr, yo = xdtt[t % 2], xdd[t % 2], xdwt[t % 2], yt[t % 2], szt[t % 2], nrt[t % 2], yot[t % 3]
                    S_ = sm.ap
                    pst = PS[2]
                    for b in range(6):
                        kb.tr(pst.bf[:, b * 128:(b + 1) * 128], xa_.ap[:, b, cs_:cs_ + 128], ident.ap, rd=[xa_], wr=[pst])
                    kb.cp("act", xb.ap, pst.bf[:, 0:768], rd=[pst], wr=[xb])
                    yield
                    kb.tt("dve", S_[:, 0:8], dtr.ap, dtb.ap, ALU.add, rd=[dtr, dtb], wr=[sm])
                    kb.act(S_[:, 0:8], S_[:, 0:8], AF.Exp, rd=[], wr=[sm])
                    kb.act(S_[:, 0:8], S_[:, 0:8], AF.Ln, rd=[], wr=[sm], bias=1.0)
                    kb.tt("dve", S_[:, 8:16], S_[:, 0:8], abc.ap, ALU.mult, rd=[abc], wr=[sm])
                    pss = PS[3]
                    hl = hlt[t % 2]
                    split_hl(hl.ap[:, 0:8], hl.ap[:, 8:16], S_[:, 8:16], rd=[sm], wr=[hl])
                    kb.mm(pss.ap[:, 0:8], tri.ap, hl.ap[:, 0:8], True, False, rd=[hl], wr=[pss], sig=False)
                    kb.mm(pss.ap[:, 0:8], tri.ap, hl.ap[:, 8:16], False, True, rd=[hl], wr=[pss], sig=False)
                    kb.mm(pss.ap[:, 8:16], onesb.ap, hl.ap[:, 0:8], False, False, rd=[hl], wr=[pss], sig=False)
                    kb.mm(pss.ap[:, 8:16], onesb.ap, hl.ap[:, 8:16], False, True, rd=[hl], wr=[pss])
                    kb.cp("dve", S_[:, 16:32], pss.ap[:, 0:16], rd=[pss], wr=[sm])
                    yield
                    kb.cp("dve", S_[:, 32:40], S_[:, 16:24], rd=[], wr=[sm])
                    kb.tt("dve", S_[:, 40:48], S_[:, 24:32], S_[:, 16:24], ALU.subtract, rd=[], wr=[sm])
                    kb.cp("dve", S_[:, 48:56], S_[:, 24:32], rd=[], wr=[sm])
                    kb.act(S_[:, 56:80], S_[:, 32:56], AF.Exp, rd=[], wr=[sm])
                    kb.cp("dve", dab.ap[:, 0], hl.ap[:, 0:8].unsqueeze(2).to_broadcast([128, 8, 128]), rd=[hl], wr=[dab])
                    kb.cp("dve", dab.ap[:, 1], hl.ap[:, 8:16].unsqueeze(2).to_broadcast([128, 8, 128]), rd=[hl], wr=[dab])
                    psc = (PS[4], PS[5])
                    for h in range(8):
                        cols = psc[h // 4].ap[:, (h % 4) * 128:(h % 4 + 1) * 128]
                        kb.mm(cols, dab.ap[:, 0, h, :], tri.ap, (h % 4 == 0), False, rd=[dab], wr=[psc[h // 4]], sig=False)
                        kb.mm(cols, dab.ap[:, 1, h, :], tri.ap, False, True, rd=[dab], wr=[psc[h // 4]])
                    for h in range(8):
                        kb.ts("dve", seg.ap[:, h, :], psc[h // 4].ap[:, (h % 4) * 128:(h % 4 + 1) * 128], S_[:, 16 + h:17 + h], 0.0,
                              ALU.subtract, ALU.min, rd=[psc[h // 4], sm], wr=[seg])
                    yield
                    kb.act(seg.ap, seg.ap, AF.Exp, rd=[], wr=[seg])
                    psb = PS[6]
                    for g in range(2):
                        kb.mm(psb.ap[:, g * 128:(g + 1) * 128], xa_.ap[:, 4 + g, cs_:cs_ + 128], xa_.ap[:, 6 + g, cs_:cs_ + 128],
                              True, True, rd=[xa_], wr=[psb])
                    kb.tt("dve", cbm.ap, psb.ap[:, 0:256].rearrange("p (g l) -> p g l", g=2),
                          trif.ap.unsqueeze(1).to_broadcast([128, 2, 128]), ALU.mult, rd=[psb, trif], wr=[cbm])
                    yield
                    while done[0] < t:
                        yield
                    kb.tt("dve", MT.ap.rearrange("p (g r) l -> p g r l", g=2), seg.ap.rearrange("p (g r) l -> p g r l", g=2),
                          cbm.ap.unsqueeze(2).to_broadcast([128, 2, 4, 128]), ALU.mult, rd=[seg, cbm], wr=[MT])
                    xv = xb.ap[:, 0:512].rearrange("p (h d) -> p h d", h=8)
                    kb.tt("dve", xdt.ap, xv, S_[:, 0:8].unsqueeze(2).to_broadcast([128, 8, 64]), ALU.mult, rd=[xb, sm], wr=[xdt])
                    kb.tt("pool", xd.ap, xv, dbc.ap, ALU.mult, rd=[xb, dbc], wr=[xd])
                    kb.tt("pool", xdw.ap, xdt.ap, S_[:, 64:72].unsqueeze(2).to_broadcast([128, 8, 64]), ALU.mult, rd=[xdt, sm], wr=[xdw])
                    psy = PS[7]
                    kb.mm(psy.ap, ident.ap, xd.ap.rearrange("p h d -> p (h d)"), True, False, rd=[xd], wr=[psy], sig=False)
                    for h in range(8):
                        kb.mm(psy.ap[:, h * 64:(h + 1) * 64], MT.ap[:, h, :], xdt.ap[:, h, :], False, h == 7, rd=[MT, xdt], wr=[psy])
                    psyo = PS[0]
                    for g in range(2):
                        kb.mm(psyo.ap[:, g * 256:(g + 1) * 256], xa_.ap[:, 6 + g, cs_:cs_ + 128], prev.ap[:, g * 256:(g + 1) * 256],
                              True, True, rd=[xa_, prev], wr=[psyo])
                    kb.tt("dve", y.ap.rearrange("p (h d) -> p h d", h=8), psyo.ap.rearrange("p (h d) -> p h d", h=8),
                          S_[:, 56:64].unsqueeze(2).to_broadcast([128, 8, 64]), ALU.mult, rd=[psyo, sm], wr=[y])
                    kb.tt("dve", y.ap, y.ap, psy.ap, ALU.add, rd=[psy], wr=[y])
                    yield
                    kb.act(sz.ap, z.ap, AF.Silu, rd=[z], wr=[sz])
                    kb.tt("pool", y.ap, y.ap, sz.ap, ALU.mult, rd=[sz], wr=[y])
                    for g in range(2):
                        kb.act(junk.ap[:, 0:256], y.ap[:, g * 256:(g + 1) * 256], AF.Square, rd=[y], wr=[junk, nr],
                               accum_out=nr.ap[:, g:g + 1])
                    rstd_from_ss(nr.ap[:, 2:4], nr.ap[:, 0:2], 256, rd=[], wr=[nr])
                    for g in range(2):
                        kb.stt("dve", yo.ap[:, g * 256:(g + 1) * 256], y.ap[:, g * 256:(g + 1) * 256], nr.ap[:, 2 + g:3 + g],
                               gn.ap[:, g * 256:(g + 1) * 256], ALU.mult, ALU.mult, rd=[y, nr, gn], wr=[yo])
                    kb.dma((MIX[s, t * 128:(t + 1) * 128, 512:1024], yo.ap), slot=yo, rd=[yo])
                    psst = PS[1]
                    xdwf = xdw.ap.rearrange("p h d -> p (h d)")
                    for g in range(2):
                        kb.mm(psst.ap[:, g * 256:(g + 1) * 256], xb.ap[:, 512 + g * 128:512 + (g + 1) * 128],
                              xdwf[:, g * 256:(g + 1) * 256], True, True, rd=[xb, xdw], wr=[psst])
                    kb.tt("dve", H.ap, H.ap, S_[:, 72:80].unsqueeze(2).to_broadcast([128, 8, 64]), ALU.mult, rd=[sm], wr=[H])
                    Hf = H.ap.rearrange("p h d -> p (h d)")
                    kb.tt("dve", Hf, Hf, psst.ap, ALU.add, rd=[psst], wr=[H])
                    kb.cp("act", prev.ap, Hf, rd=[H], wr=[prev])
                    done[0] += 1

            yield from seq_driver([(lambda U=U, cc=cc: chunk_gen(U, cc)) for U in range(NU) for cc in range(4)])

        gens = [[gen(s_), SEQ_SKEW * s_] for s_ in range(NS)]
        while gens:
            for g_ in list(gens):
                if g_[1] > 0:
                    g_[1] -= 1
                    continue
                try:
                    next(g_[0])
                except StopIteration:
                    gens.remove(g_)
        kb.barrier()

    def phase_gla(L, fused_s=None):
        if fused_s is None:
            if FUSE_GLA:
                return None
            A.off = MARK
        B = PS if fused_s is None else [PS[6], PS[6], PS[7], PS[6], PS[7], PS[6], PS[7], PS[6]]

        def gen(s):
            wgf = A.alloc([32, 256], F32)
            wg = A.alloc([32, 256], BF16)
            gnb = A.alloc([128, 128], F32)
            P = A.alloc([128, 2, 128], F32)
            prevb = A.alloc([128, 4, 128], BF16)
            kb.memset("dve", wgf.ap, 0.0, wr=[wgf])
            kb.dma((wgf.ap[0:16, :], gla_w_gate2[L]), slot=wgf, wr=[wgf])
            kb.dma((wgf.ap[16:17, :], gla_b_gate[L].rearrange("(o n) -> o n", o=1)), slot=wgf, wr=[wgf])
            kb.cp("dve", wg.ap, wgf.ap, rd=[wgf], wr=[wg])
            bcast_load(gnb, gla_norm[L], mul=math.sqrt(128.0))
            kb.memset("dve", P.ap, 0.0, wr=[P])
            kb.memset("pool", prevb.ap, 0.0, wr=[prevb])
            lrt = [A.alloc([32, 128], BF16) for _ in range(3)]
            for l_ in lrt:
                kb.memset("pool", l_.ap, 1.0, wr=[l_])
            gin = [A.alloc([128, 1536], BF16) for _ in range(3)]
            ut = [A.alloc([128, 256], F32) for _ in range(2)]
            uht = [A.alloc([128, 2, 256], BF16) for _ in range(2)]
            eet = [A.alloc([128, 3, 256], F32) for _ in range(2)]
            qdt = [A.alloc([128, 256], BF16) for _ in range(2)]
            kit = [A.alloc([128, 256], BF16) for _ in range(2)]
            ket = [A.alloc([128, 256], BF16) for _ in range(2)]
            qkTt = [A.alloc([128, 4, 128], BF16) for _ in range(2)]
            attmt = [A.alloc([128, 4, 128], BF16) for _ in range(2)]
            kipt = [A.alloc([128, 4, 128], BF16) for _ in range(2)]
            for k_ in kipt:
                kb.memset("dve", k_.ap, 0.0, wr=[k_])
            ot = [A.alloc([128, 4, 128], F32) for _ in range(2)]
            sqt = [A.alloc([128, 4, 128], F32) for _ in range(2)]
            nrt = [A.alloc([128, 8], F32) for _ in range(2)]
            sgt = [A.alloc([128, 512], F32) for _ in range(2)]
            ygt = [A.alloc([128, 512], BF16) for _ in range(3)]
            cdt = [A.alloc([128, 2], F32) for _ in range(2)]

            def load(t):
                kb.dma((gin[t % 3].ap, PT[s, t * 128:(t + 1) * 128, 2048:3584]), slot=gin[t % 3], wr=[gin[t % 3]])
                kb.dma((lrt[t % 3].ap[0:16, :], LRT[s, :, t * 128:(t + 1) * 128]), slot=lrt[t % 3], wr=[lrt[t % 3]])

            load(0)
            done = [0]

            def tile_gen(t_only):
              for t in (t_only,):
                if t + 1 < NT:
                    load(t + 1)
                g_, lr, u, ee, qd, ki, ke = gin[t % 3], lrt[t % 3], ut[t % 2], eet[t % 2], qdt[t % 2], kit[t % 2], ket[t % 2]
                qkT, attm, o, sq_, nr, sg, yg, cd = qkTt[t % 2], attmt[t % 2], ot[t % 2], sqt[t % 2], nrt[t % 2], sgt[t % 2], ygt[t % 3], cdt[t % 2]
                if GLA_CUT <= 1:
                    continue
                psx = B[0]
                kb.mm(psx.ap[:, 0:256], lr.ap[0:32, :], wg.ap[0:32, :], True, True, rd=[lr, wg], wr=[psx])
                kb.act(u.ap, psx.ap[:, 0:256], AF.Exp, rd=[psx], wr=[u], scale=-1.0)
                kb.act(u.ap, u.ap, AF.Ln, rd=[], wr=[u], bias=1.0)
                yield
                if GLA_CUT <= 2:
                    continue
                psc, pstot = B[1], B[2]
                uh = uht[t % 2]
                split_hl(uh.ap[:, 0, :], uh.ap[:, 1, :], u.ap, rd=[u], wr=[uh])
                kb.mm(psc.ap[:, 0:256], tri.ap, uh.ap[:, 0, :], True, False, rd=[uh], wr=[psc], sig=False)
                kb.mm(psc.ap[:, 0:256], tri.ap, uh.ap[:, 1, :], False, True, rd=[uh], wr=[psc])
                kb.mm(pstot.ap[:, 0:256], onesb.ap, uh.ap[:, 0, :], True, False, rd=[uh], wr=[pstot], sig=False)
                kb.mm(pstot.ap[:, 0:256], onesb.ap, uh.ap[:, 1, :], False, True, rd=[uh], wr=[pstot])
                csb = sgt[t % 2]
                kb.cp("act", csb.ap[:, 0:256], psc.ap[:, 0:256], rd=[psc], wr=[csb])
                kb.act(ee.ap[:, 0, :], csb.ap[:, 0:256], AF.Exp, rd=[csb], wr=[ee], scale=-1.0 / 16)
                kb.act(ee.ap[:, 1, :], csb.ap[:, 0:256], AF.Exp, rd=[csb], wr=[ee], scale=1.0 / 16)
                kb.tt("dve", ee.ap[:, 2, :], pstot.ap[:, 0:256], csb.ap[:, 0:256], ALU.subtract, rd=[pstot, csb], wr=[ee])
                kb.act(ee.ap[:, 2, :], ee.ap[:, 2, :], AF.Exp, rd=[], wr=[ee], scale=-1.0 / 16)
                yield
                if GLA_CUT <= 3:
                    continue
                kb.stt("dve", qd.ap, g_.ap[:, 0:256], 0.125, ee.ap[:, 0, :], ALU.mult, ALU.mult, rd=[g_, ee], wr=[qd])
                kb.tt("pool", ki.ap, g_.ap[:, 256:512], ee.ap[:, 1, :], ALU.mult, rd=[g_, ee], wr=[ki])
                kb.tt("pool", ke.ap, g_.ap[:, 256:512], ee.ap[:, 2, :], ALU.mult, rd=[g_, ee], wr=[ke])
                pst = B[3]
                for j in range(2):
                    kb.tr(pst.bf[:, j * 128:(j + 1) * 128], qd.ap[:, j * 128:(j + 1) * 128], ident.ap, rd=[qd], wr=[pst])
                for j in range(2):
                    kb.tr(pst.bf[:, (2 + j) * 128:(3 + j) * 128], ki.ap[:, j * 128:(j + 1) * 128], ident.ap, rd=[ki], wr=[pst])
                kb.cp("act", qkT.ap[:, 0:2, :].rearrange("p a b -> p (a b)"), pst.bf[:, 0:256], rd=[pst], wr=[qkT])
                kip = kipt[t % 2]
                for j in range(2):
                    for i in range(2):
                        kb.cp("act", kip.ap[i * 64:(i + 1) * 64, 2 * j + i, :], pst.bf[i * 64:(i + 1) * 64, (2 + j) * 128:(3 + j) * 128],
                              rd=[pst], wr=[kip])
                yield
                if GLA_CUT <= 4:
                    continue
                psatt = B[4]
                for h in range(4):
                    j, i = h // 2, h % 2
                    kb.mm(psatt.ap[:, h * 128:(h + 1) * 128], kip.ap[:, h, :], qkT.ap[:, j, :],
                          True, True, rd=[qkT, kip], wr=[psatt])
                if GLA_SUB <= 1:
                    continue
                kb.tt("dve", attm.ap, psatt.ap.rearrange("p (h l) -> p h l", h=4), trif.ap.unsqueeze(1).to_broadcast([128, 4, 128]),
                      ALU.mult, rd=[psatt, trif], wr=[attm])
                yield
                while done[0] < t:
                    yield
                if GLA_SUB <= 2:
                    continue
                pso = B[5]
                for h in range(4):
                    j, i = h // 2, h % 2
                    kb.mm(pso.ap[:, h * 128:(h + 1) * 128], attm.ap[:, h, :], g_.ap[:, 512 + h * 128:512 + (h + 1) * 128], True, False,
                          rd=[attm, g_], wr=[pso], sig=False)
                    kb.mm(pso.ap[:, h * 128:(h + 1) * 128], qkT.ap[:, j, :], prevb.ap[:, h, :],
                          False, True, rd=[qkT, prevb], wr=[pso])
                if GLA_SUB <= 3:
                    continue
                of = o.ap.rearrange("p a b -> p (a b)")
                kb.cp("act", of, pso.ap, rd=[pso], wr=[o])
                yield
                if GLA_CUT <= 5:
                    continue
                kb.tt("pool", sq_.ap, o.ap, o.ap, ALU.mult, rd=[o], wr=[sq_])
                kb.red("dve", nr.ap[:, 0:4], sq_.ap, rd=[sq_], wr=[nr])
                rstd_from_ss(nr.ap[:, 4:8], nr.ap[:, 0:4], 128, rd=[], wr=[nr])
                kb.act(sg.ap, g_.ap[:, 1024:1536], AF.Silu, rd=[g_], wr=[sg])
                kb.tt("dve", o.ap, o.ap, nr.ap[:, 4:8].unsqueeze(2).to_broadcast([128, 4, 128]), ALU.mult, rd=[nr], wr=[o])
                kb.tt("pool", o.ap, o.ap, gnb.ap.unsqueeze(1).to_broadcast([128, 4, 128]), ALU.mult, rd=[gnb], wr=[o])
                kb.tt("dve", yg.ap, of, sg.ap, ALU.mult, rd=[o, sg], wr=[yg])
                kb.dma((MIX[s, t * 128:(t + 1) * 128, 1024:1536], yg.ap), slot=yg, rd=[yg])
                yield
                if GLA_CUT <= 6:
                    continue
                pss_ = B[6]
                for j in range(2):
                    kb.mm(pss_.ap[:, j * 256:(j + 1) * 256], ke.ap[:, j * 128:(j + 1) * 128],
                          g_.ap[:, 512 + j * 256:512 + (j + 1) * 256], True, True, rd=[ke, g_], wr=[pss_])
                pscd = B[7]
                for j in range(2):
                    kb.mm(pscd.ap[:, j * 8:j * 8 + 8], uh.ap[:, 0, j * 128:(j + 1) * 128], onesb.ap[:, 0:8], (j == 0), False, rd=[uh], wr=[pscd], sig=False)
                    kb.mm(pscd.ap[:, j * 8:j * 8 + 8], uh.ap[:, 1, j * 128:(j + 1) * 128], onesb.ap[:, 0:8], False, True, rd=[uh], wr=[pscd])
                kb.act(cd.ap, pscd.ap[:, 0:16].rearrange("p (j e) -> p j e", j=2)[:, :, 0], AF.Exp, rd=[pscd], wr=[cd], scale=-1.0 / 16)
                for j in range(2):
                    for i in range(2):
                        kb.stt("dve", P.ap[i * 64:(i + 1) * 64, j, :], P.ap[i * 64:(i + 1) * 64, j, :], cd.ap[i * 64:(i + 1) * 64, j:j + 1],
                               pss_.ap[i * 64:(i + 1) * 64, j * 256 + i * 128:j * 256 + (i + 1) * 128], ALU.mult, ALU.add,
                               rd=[cd, pss_], wr=[P])
                for j in range(2):
                    for i in range(2):
                        kb.cp("act", prevb.ap[i * 64:(i + 1) * 64, 2 * j + i, :], P.ap[i * 64:(i + 1) * 64, j, :], rd=[P], wr=[prevb])
                done[0] += 1

            yield from seq_driver([(lambda t=t: tile_gen(t)) for t in range(NT)])

        if fused_s is not None:
            return gen(fused_s)
        gens = [[gen(s_), SEQ_SKEW * s_] for s_ in range(NS)]
        while gens:
            for g_ in list(gens):
                if g_[1] > 0:
                    g_[1] -= 1
                    continue
                try:
                    next(g_[0])
                except StopIteration:
                    gens.remove(g_)
        kb.barrier()

    def phase_a3(L, hsrc):
        A.off = MARK
        Wo = A.alloc([128, 12, D], BF16)
        stage = [A.alloc([128, 1024], F32) for _ in range(3)]
        load_w(Wo, w_out[L], 12, D, stage)
        kb.barrier()
        mixt = [A.alloc([128, 1536], BF16) for _ in range(3)]
        hin = [A.alloc([128, D], F32) for _ in range(3)]
        mT = [A.alloc([128, 12, 128], BF16) for _ in range(2)]
        h1 = [A.alloc([128, D], F32) for _ in range(3)]
        tiles = [(s, t) for s in range(NS) for t in range(NT)]

        def load(g):
            s, t = tiles[g]
            kb.dma((mixt[g % 3].ap, MIX[s, t * 128:(t + 1) * 128, :]), slot=mixt[g % 3], wr=[mixt[g % 3]])
            kb.dma((hin[g % 3].ap, hsrc[s, t * 128:(t + 1) * 128, :]), slot=hin[g % 3], wr=[hin[g % 3]])

        load(0)
        for g, (s, t) in enumerate(tiles):
            if g + 1 < len(tiles):
                load(g + 1)
            mx, h, m_, ho = mixt[g % 3], hin[g % 3], mT[g % 2], h1[g % 3]
            pa, pb = PS[0], PS[1]
            for k in range(12):
                dst = pa.bf[:, k * 128:(k + 1) * 128] if k < 8 else pb.bf[:, (k - 8) * 128:(k - 7) * 128]
                kb.tr(dst, mx.ap[:, k * 128:(k + 1) * 128], ident.ap, rd=[mx], wr=[pa if k < 8 else pb])
            kb.cp("act", m_.ap[:, 0:8, :], pa.bf.rearrange("p (a b) -> p a b", a=8), rd=[pa], wr=[m_])
            kb.cp("dve", m_.ap[:, 8:12, :], pb.bf[:, 0:512].rearrange("p (a b) -> p a b", a=4), rd=[pb], wr=[m_])
            for n in range(2):
                ps = PS[2 + (2 * g + n) % 4]
                for k in range(12):
                    kb.mm(ps.ap, m_.ap[:, k, :], Wo.ap[:, k, n * 512:(n + 1) * 512], k == 0, k == 11, rd=[m_], wr=[ps])
                kb.tt("dve", ho.ap[:, n * 512:(n + 1) * 512], ps.ap, h.ap[:, n * 512:(n + 1) * 512], ALU.add, rd=[ps, h], wr=[ho])
            kb.dma((out[s, t * 128:(t + 1) * 128, :], ho.ap), slot=ho, rd=[ho])
        kb.barrier()

    def phase_ffn(L):
        A.off = MARK
        Wg = A.alloc([128, 8, DFF], BF16)
        Wu = A.alloc([128, 8, DFF], BF16)
        Wd = A.alloc([128, 22, D], BF16)
        gain = A.alloc([128, D], F32)
        mark2 = A.off
        stage = [A.alloc([128, 1024], F32) for _ in range(2)]
        load_w(Wg, w_ffn_gate[L], 8, DFF, stage)
        load_w(Wu, w_ffn_up[L], 8, DFF, stage)
        load_w(Wd, w_ffn_down[L], 22, D, stage)
        bcast_load(gain, ffn_norm[L], mul=32.0)
        kb.barrier()
        A.off = mark2
        NJ = 4
        hin = [A.alloc([128, D], F32) for _ in range(4)]
        fb = [A.alloc([128, D], BF16) for _ in range(2)]
        fT = [A.alloc([128, 8, 128 * NJ], BF16) for _ in range(2)]
        gT = A.alloc([128, 22, 128 * NJ], BF16)
        sgb = [A.alloc([128, 128 * NJ], BF16) for _ in range(2)]
        ss = [A.alloc([128, 2], F32) for _ in range(4)]
        tiles = [(s, t) for s in range(NS) for t in range(NT)]
        W_ = 128 * NJ

        def load(g):
            s, t = tiles[g]
            kb.dma((hin[g % 4].ap, out[s, t * 128:(t + 1) * 128, :]), slot=hin[g % 4], wr=[hin[g % 4]])

        for g in range(4):
            load(g)
        for u in range(len(tiles) // NJ):
            ft = fT[u % 2]
            for jj in range(NJ):
                g = NJ * u + jj
                h, sq, f_ = hin[g % 4], ss[g % 4], fb[g % 2]
                kb.act(junk.ap, h.ap, AF.Square, rd=[h], wr=[junk, sq], accum_out=sq.ap[:, 0:1])
                rstd_from_ss(sq.ap[:, 1:2], sq.ap[:, 0:1], D, rd=[], wr=[sq])
                kb.stt("dve", f_.ap, h.ap, sq.ap[:, 1:2], gain.ap, ALU.mult, ALU.mult, rd=[h, sq, gain], wr=[f_])
                pst = PS[g % 2]
                for k in range(8):
                    kb.tr(pst.bf[:, k * 128:(k + 1) * 128], f_.ap[:, k * 128:(k + 1) * 128], ident.ap, rd=[f_], wr=[pst])
                evac(ft.ap[:, :, jj * 128:(jj + 1) * 128], pst.bf.rearrange("p (a b) -> p a b", a=8), rd=[pst], wr=[ft])
            for hb in range(22):
                psg, psu = PS[2 + (hb % 2) * 2], PS[3 + (hb % 2) * 2]
                for k in range(8):
                    kb.mm(psg.ap[:, 0:W_], Wg.ap[:, k, hb * 128:(hb + 1) * 128], ft.ap[:, k, :], k == 0, k == 7, rd=[ft], wr=[psg])
                for k in range(8):
                    kb.mm(psu.ap[:, 0:W_], Wu.ap[:, k, hb * 128:(hb + 1) * 128], ft.ap[:, k, :], k == 0, k == 7, rd=[ft], wr=[psu])
                sg_ = sgb[hb % 2]
                kb.act(sg_.ap, psg.ap[:, 0:W_], AF.Silu, rd=[psg], wr=[sg_])
                kb.tt("dve", gT.ap[:, hb, :], sg_.ap, psu.ap[:, 0:W_], ALU.mult, rd=[sg_, psu], wr=[gT])
            for jj in range(NJ):
                g = NJ * u + jj
                s, t = tiles[g]
                h = hin[g % 4]
                for n in range(2):
                    ps = PS[6 + n]
                    for k2 in range(22):
                        kb.mm(ps.ap, gT.ap[:, k2, jj * 128:(jj + 1) * 128], Wd.ap[:, k2, n * 512:(n + 1) * 512], k2 == 0, k2 == 21,
                              rd=[gT], wr=[ps])
                    kb.tt("dve", h.ap[:, n * 512:(n + 1) * 512], ps.ap, h.ap[:, n * 512:(n + 1) * 512], ALU.add, rd=[ps], wr=[h])
                kb.dma((out[s, t * 128:(t + 1) * 128, :], h.ap), slot=h, rd=[h])
                if g + 4 < len(tiles):
                    load(g + 4)
        kb.barrier()

    def phase_ple(L):
        A.off = MARK
        Wpg = A.alloc([128, 8, D], BF16)
        Wpp = A.alloc([128, 2, D], BF16)
        stage = [A.alloc([128, 1024], F32) for _ in range(3)]
        load_w(Wpg, ple_w_gate[L], 8, D, stage)
        load_w(Wpp, ple_w_proj[L], 2, D, stage)
        kb.barrier()
        hin = [A.alloc([128, D], F32) for _ in range(3)]
        pin = [A.alloc([128, 256], F32) for _ in range(3)]
        hb = [A.alloc([128, D], BF16) for _ in range(2)]
        pb = [A.alloc([128, 256], BF16) for _ in range(2)]
        hT = [A.alloc([128, 8, 128], BF16) for _ in range(2)]
        pT = [A.alloc([128, 2, 128], BF16) for _ in range(2)]
        sgt = [A.alloc([128, 512], F32) for _ in range(2)]
        tiles = [(s, t) for s in range(NS) for t in range(NT)]

        def load(g):
            s, t = tiles[g]
            kb.dma((hin[g % 3].ap, out[s, t * 128:(t + 1) * 128, :]), slot=hin[g % 3], wr=[hin[g % 3]])
            kb.dma((pin[g % 3].ap, p[L, s, t * 128:(t + 1) * 128, :]), slot=pin[g % 3], wr=[pin[g % 3]])

        load(0)
        for g, (s, t) in enumerate(tiles):
            if g + 1 < len(tiles):
                load(g + 1)
            h, pi_, hb_, pb_, hT_, pT_ = hin[g % 3], pin[g % 3], hb[g % 2], pb[g % 2], hT[g % 2], pT[g % 2]
            kb.cp("act", hb_.ap, h.ap, rd=[h], wr=[hb_])
            kb.cp("pool", pb_.ap, pi_.ap, rd=[pi_], wr=[pb_])
            pst, pst2 = PS[g % 2], PS[2 + g % 2]
            for k in range(8):
                kb.tr(pst.bf[:, k * 128:(k + 1) * 128], hb_.ap[:, k * 128:(k + 1) * 128], ident.ap, rd=[hb_], wr=[pst])
            for k in range(2):
                kb.tr(pst2.bf[:, k * 128:(k + 1) * 128], pb_.ap[:, k * 128:(k + 1) * 128], ident.ap, rd=[pb_], wr=[pst2])
            kb.cp("dve", hT_.ap.rearrange("p a b -> p (a b)"), pst.bf, rd=[pst], wr=[hT_])
            kb.cp("act", pT_.ap.rearrange("p a b -> p (a b)"), pst2.bf[:, 0:256], rd=[pst2], wr=[pT_])
            for n in range(2):
                psg, psp = PS[4 + n], PS[6 + n]
                sg = sgt[(2 * g + n) % 2]
                for k in range(8):
                    kb.mm(psg.ap, hT_.ap[:, k, :], Wpg.ap[:, k, n * 512:(n + 1) * 512], k == 0, k == 7, rd=[hT_], wr=[psg])
                for k in range(2):
                    kb.mm(psp.ap, pT_.ap[:, k, :], Wpp.ap[:, k, n * 512:(n + 1) * 512], k == 0, k == 1, rd=[pT_], wr=[psp])
                kb.act(sg.ap, psg.ap, AF.Sigmoid, rd=[psg], wr=[sg])
                kb.tt("dve", sg.ap, sg.ap, psp.ap, ALU.mult, rd=[psp], wr=[sg])
                kb.tt("dve", h.ap[:, n * 512:(n + 1) * 512], h.ap[:, n * 512:(n + 1) * 512], sg.ap, ALU.add, rd=[sg], wr=[h])
            kb.dma((out[s, t * 128:(t + 1) * 128, :], h.ap), slot=h, rd=[h])
        kb.barrier()

    def phase_gla_stub(L):
        A.off = MARK
        z = A.alloc([128, 512], BF16)
        kb.memset("dve", z.ap, 0.0, wr=[z])
        for s in range(NS):
            for t in range(NT):
                kb.dma((MIX[s, t * 128:(t + 1) * 128, 1024:1536], z.ap), slot=z, rd=[z])
        kb.barrier()

    if os.environ.get("GLA_REAL", "1") != "1":
        phase_gla = phase_gla_stub
    return dict(nc=nc, kb=kb, a1=phase_a1, da=phase_da, ssd=phase_ssd, gla=phase_gla, a3=phase_a3, ffn=phase_ffn, ple=phase_ple,
                x=x, out=out)


def build_full(S, NS, NL, dbg=False, phases=None):
    b = build(S, NS, NL, dbg=dbg)
    for L in range(NL):
        hsrc = b["x"] if L == 0 else b["out"]
        for name in ("a1", "da", "ssd", "gla", "a3", "ffn", "ple"):
            if phases is not None and name not in phases:
                continue
            if name in ("a1", "a3"):
                b[name](L, hsrc)
            else:
                b[name](L)
    b["kb"].barrier()
    b["kb"].emit()
    return b["nc"], b["kb"]


def make_consts():
    tri = np.triu(np.ones((128, 128), np.float32))
    invf = (500000.0 ** (-np.arange(0, 16, 2, dtype=np.float32) / 16.0)).astype(np.float32)
    return {
        "c_ident": np.eye(128, dtype=np.float32).astype(ml_dtypes.bfloat16),
        "c_tri": tri.astype(ml_dtypes.bfloat16),
        "c_trif": tri,
        "c_onesf": np.ones((128, 128), np.float32),
        "c_onesb": np.ones((128, 128), np.float32).astype(ml_dtypes.bfloat16),
        "c_invf": np.ascontiguousarray(np.broadcast_to(invf[None, :], (128, 8))),
    }


WNAMES = ["attn_norm", "w_in", "da_q_norm", "da_k_norm", "da_lambda_q1", "da_lambda_k1", "da_lambda_q2", "da_lambda_k2",
          "da_sub_norm", "ssd_conv_w", "ssd_conv_b", "ssd_dt_bias", "ssd_a_log", "ssd_d", "ssd_norm", "gla_w_gate2",
          "gla_b_gate", "gla_norm", "w_out", "ffn_norm", "w_ffn_gate", "w_ffn_up", "w_ffn_down", "ple_w_proj", "ple_w_gate"]

_CACHE = {}


def run(inputs, S, NS, NL, ncores, dbg=False, phases=None):
    key = (S, NS, NL, dbg, tuple(phases) if phases else None)
    if key not in _CACHE:
        _CACHE[key] = build_full(S, NS, NL, dbg=dbg, phases=phases)
    nc, kb = _CACHE[key]
    consts = make_consts()
    in_maps = []
    for c in range(ncores):
        m = dict(consts)
        m["x"] = np.ascontiguousarray(np.asarray(inputs["x"], np.float32)[c * NS:(c + 1) * NS])
        m["p"] = np.ascontiguousarray(np.asarray(inputs["p"], np.float32)[:NL, c * NS:(c + 1) * NS])
        m["positions"] = np.ascontiguousarray(np.asarray(inputs["positions"], np.int32)[c * NS:(c + 1) * NS])
        for w in WNAMES:
            m[w] = np.ascontiguousarray(np.asarray(inputs[w], np.float32)[:NL])
        in_maps.append(m)
    res = run_bass_kernel_spmd(nc, in_maps, core_ids=list(range(ncores)))
    return res


def kernel(**inputs):
    res = run(inputs, 4096, 2, 4, 8)
    return np.concatenate([np.asarray(r["out"], np.float32) for r in res.results], axis=0)
```
